# Optimizing a Trainium2 kernel written in Bass

```python
import jax, jax.numpy as jnp
from jax import lax
import numpy as np

D_MODEL = 1024
BATCH = 1
SEQ = 16384
DEPTH = 1

D_MIX = D_MODEL
DN_HEADS = 4
DN_HEAD_DIM = D_MODEL // 8
DN_WIDTH = DN_HEADS * DN_HEAD_DIM
CONV_K = 4
CHUNK = 64
AT_HEADS = 8
AT_HEAD_DIM = D_MODEL // 16
AT_WIDTH = AT_HEADS * AT_HEAD_DIM
PATTERNS = ((128, 1), (512, 4), (2048, 16))
Q_BLOCK = 128
ROPE_THETA = 10000.0
EPS = 1e-6
SPLIT_SIZES = (3 * DN_WIDTH, DN_WIDTH, DN_HEADS, DN_HEADS, AT_WIDTH, AT_WIDTH, AT_WIDTH, AT_WIDTH)
IN_COLS = 4 * DN_WIDTH + 2 * DN_HEADS + 4 * AT_WIDTH

kernel_name = "hybrid_deltanet_dilated_swa_adaln"


def rmsnorm(x, w):
    xf = x.astype(jnp.float32)
    xf = xf * lax.rsqrt(jnp.mean(xf * xf, axis=-1, keepdims=True) + EPS)
    return xf.astype(x.dtype) * w


def l2norm(x):
    xf = x.astype(jnp.float32)
    return xf * lax.rsqrt(jnp.sum(xf * xf, axis=-1, keepdims=True) + EPS)


def rope(x, positions):
    hd = x.shape[-1]
    half = hd // 2
    inv_freq = ROPE_THETA ** (-jnp.arange(half, dtype=jnp.float32) / half)
    ang = positions.astype(jnp.float32)[..., None] * inv_freq
    cos = jnp.cos(ang)[:, :, None, :]
    sin = jnp.sin(ang)[:, :, None, :]
    x1, x2 = x[..., :half], x[..., half:]
    out = jnp.concatenate([x1 * cos - x2 * sin, x2 * cos + x1 * sin], axis=-1)
    return out.astype(x.dtype)


def causal_short_conv(x, w):
    K = w.shape[0]
    S = x.shape[1]
    xp = jnp.pad(x, ((0, 0), (K - 1, 0), (0, 0)))
    out = xp[:, 0:S] * w[0]
    for j in range(1, K):
        out = out + xp[:, j:j + S] * w[j]
    return out


def gated_delta_rule(q, k, v, g, beta):
    B, S, H, dk = q.shape
    dv = v.shape[-1]
    nc = S // CHUNK

    def chunks4(t):
        return t.reshape(B, nc, CHUNK, H, t.shape[-1]).transpose(0, 3, 1, 2, 4)

    def chunks3(t):
        return t.reshape(B, nc, CHUNK, H).transpose(0, 3, 1, 2)

    q = chunks4(q) * (dk ** -0.5)
    k = chunks4(k)
    v = chunks4(v)
    beta = chunks3(beta)
    gc = jnp.cumsum(chunks3(g), axis=-1)

    tril = jnp.tril(jnp.ones((CHUNK, CHUNK), dtype=bool))
    strict = tril & ~jnp.eye(CHUNK, dtype=bool)
    decay_mat = jnp.exp(jnp.where(tril, gc[..., :, None] - gc[..., None, :], -jnp.inf))

    kb = k * beta[..., None]
    vb = v * beta[..., None]
    a_low = jnp.where(strict, jnp.einsum('bhncd,bhnsd->bhncs', kb, k) * decay_mat, 0.0)
    ia = a_low + jnp.eye(CHUNK, dtype=jnp.float32)
    u = lax.linalg.triangular_solve(ia, vb, left_side=True, lower=True, unit_diagonal=True)
    w = lax.linalg.triangular_solve(ia, kb * jnp.exp(gc)[..., None],
                                    left_side=True, lower=True, unit_diagonal=True)
    attn_intra = jnp.where(tril, jnp.einsum('bhncd,bhnsd->bhncs', q, k) * decay_mat, 0.0)
    q_dec = q * jnp.exp(gc)[..., None]
    k_dec = k * jnp.exp(gc[..., -1:] - gc)[..., None]
    g_last = jnp.exp(gc[..., -1])

    xs = (jnp.moveaxis(u, 2, 0), jnp.moveaxis(w, 2, 0), jnp.moveaxis(q_dec, 2, 0),
          jnp.moveaxis(k_dec, 2, 0), jnp.moveaxis(attn_intra, 2, 0), jnp.moveaxis(g_last, 2, 0))

    def step(state, inp):
        u_n, w_n, qd_n, kd_n, at_n, gl_n = inp
        v_new = u_n - jnp.einsum('bhcd,bhde->bhce', w_n, state)
        o = jnp.einsum('bhcd,bhde->bhce', qd_n, state) + jnp.einsum('bhcs,bhse->bhce', at_n, v_new)
        state = state * gl_n[..., None, None] + jnp.einsum('bhcd,bhce->bhde', kd_n, v_new)
        return state, o

    s0 = jnp.zeros((B, H, dk, dv), dtype=jnp.float32)
    _, o = lax.scan(step, s0, xs)
    return o.transpose(1, 0, 3, 2, 4).reshape(B, S, H, dv)


def strided_window_attention(q, k, v, dilation, w_sub):
    B, S, H, hd = q.shape
    L = S // dilation
    nb = -(-L // Q_BLOCK)
    Lp = nb * Q_BLOCK

    def split(t):
        t = t.reshape(B, L, dilation, H, hd).transpose(0, 2, 1, 3, 4)
        return jnp.pad(t, ((0, 0), (0, 0), (0, Lp - L), (0, 0), (0, 0)))

    def band(t):
        tp = jnp.pad(t, ((0, 0), (0, 0), (Q_BLOCK, 0), (0, 0), (0, 0)))
        tp = tp.reshape(B, dilation, nb + 1, Q_BLOCK, H, hd)
        return jnp.concatenate([tp[:, :, :-1], tp[:, :, 1:]], axis=3)

    qb = split(q).reshape(B, dilation, nb, Q_BLOCK, H, hd)
    kb = band(split(k))
    vb = band(split(v))

    s = jnp.einsum('bdnqhe,bdnkhe->bdnhqk', qb, kb).astype(jnp.float32) * (hd ** -0.5)
    qi = jnp.arange(Q_BLOCK)[:, None]
    kj = jnp.arange(2 * Q_BLOCK)[None, :]
    rel = Q_BLOCK + qi - kj
    kidx = jnp.arange(nb)[:, None] * Q_BLOCK - Q_BLOCK + jnp.arange(2 * Q_BLOCK)[None, :]
    mask = ((rel >= 0) & (rel <= w_sub))[None] & (kidx >= 0)[:, None, :]
    s = jnp.where(mask[None, None, :, None], s, -jnp.inf)
    m = jnp.max(s, axis=-1, keepdims=True)
    p = jnp.exp(s - m)
    l = jnp.sum(p, axis=-1, keepdims=True)
    o = jnp.einsum('bdnhqk,bdnkhe->bdnqhe', p / l, vb.astype(jnp.float32))
    lse = (m + jnp.log(l))[..., 0]

    o = o.reshape(B, dilation, Lp, H, hd)[:, :, :L].transpose(0, 2, 1, 3, 4).reshape(B, S, H, hd)
    lse = lse.transpose(0, 1, 2, 4, 3).reshape(B, dilation, Lp, H)[:, :, :L]
    lse = lse.transpose(0, 2, 1, 3).reshape(B, S, H)
    return o, lse


def dilated_attention(q, k, v):
    outs, lses = [], []
    for window, dilation in PATTERNS:
        o, lse = strided_window_attention(q, k, v, dilation, window // dilation)
        outs.append(o)
        lses.append(lse)
    wts = jax.nn.softmax(jnp.stack(lses, axis=0), axis=0)
    return jnp.sum(wts[..., None] * jnp.stack(outs, axis=0), axis=0)


def setup_inputs(seed: int = 0) -> dict:
    key = jax.random.key(seed)
    ks = jax.random.split(key, 16)
    f32 = jnp.float32
    x = jax.random.normal(ks[0], (BATCH, SEQ, D_MODEL), f32)
    c = jax.random.normal(ks[1], (BATCH, D_MODEL), f32)
    positions = jnp.broadcast_to(jnp.arange(SEQ, dtype=jnp.int32)[None], (BATCH, SEQ))
    w_mod = jax.random.normal(ks[2], (DEPTH, D_MODEL, 3 * D_MODEL), f32) * (0.2 * D_MODEL ** -0.5)
    b_mod = jax.random.normal(ks[3], (DEPTH, 3 * D_MODEL), f32) * 0.01
    norm_w = 1.0 + 0.01 * jax.random.normal(ks[4], (DEPTH, D_MODEL), f32)
    w_in = jax.random.normal(ks[5], (DEPTH, D_MODEL, IN_COLS), f32) * (D_MODEL ** -0.5)
    conv_w = jax.random.normal(ks[6], (DEPTH, CONV_K, 3 * DN_WIDTH), f32) * (CONV_K ** -0.5)
    a_log = jnp.log(jax.random.uniform(ks[7], (DEPTH, DN_HEADS), f32, 1.0, 16.0))
    dt = jnp.exp(jax.random.uniform(ks[8], (DEPTH, DN_HEADS), f32, jnp.log(1e-3), jnp.log(1e-1)))
    dt_bias = dt + jnp.log(-jnp.expm1(-dt))
    dn_norm_w = 1.0 + 0.01 * jax.random.normal(ks[9], (DEPTH, DN_HEAD_DIM), f32)
    at_norm_w = 1.0 + 0.01 * jax.random.normal(ks[10], (DEPTH, AT_HEAD_DIM), f32)
    w_out = jax.random.normal(ks[11], (DEPTH, D_MIX, D_MODEL), f32) * (D_MIX ** -0.5)
    final_norm_w = 1.0 + 0.01 * jax.random.normal(ks[12], (D_MODEL,), f32)
    return {"x": x, "c": c, "positions": positions, "w_mod": w_mod, "b_mod": b_mod,
            "norm_w": norm_w, "w_in": w_in, "conv_w": conv_w, "a_log": a_log,
            "dt_bias": dt_bias, "dn_norm_w": dn_norm_w, "at_norm_w": at_norm_w,
            "w_out": w_out, "final_norm_w": final_norm_w}


def reference(x, c, positions, w_mod, b_mod, norm_w, w_in, conv_w, a_log, dt_bias,
              dn_norm_w, at_norm_w, w_out, final_norm_w):
    B, S, _ = x.shape
    offsets = tuple(int(o) for o in np.cumsum(SPLIT_SIZES)[:-1])
    for layer in range(DEPTH):
        mod = jax.nn.silu(c) @ w_mod[layer] + b_mod[layer]
        shift, scale, gate = jnp.split(mod, 3, axis=-1)
        h = rmsnorm(x, norm_w[layer]) * (1.0 + scale[:, None]) + shift[:, None]

        proj = h @ w_in[layer]
        dn_qkv, dn_z, dn_b, dn_a, at_q, at_k, at_v, at_z = jnp.split(proj, offsets, axis=-1)

        dn_qkv = jax.nn.silu(causal_short_conv(dn_qkv, conv_w[layer]))
        dq, dk_, dvv = jnp.split(dn_qkv, 3, axis=-1)
        dq = l2norm(dq.reshape(B, S, DN_HEADS, DN_HEAD_DIM))
        dk_ = l2norm(dk_.reshape(B, S, DN_HEADS, DN_HEAD_DIM))
        dvv = dvv.reshape(B, S, DN_HEADS, DN_HEAD_DIM).astype(jnp.float32)
        beta = jax.nn.sigmoid(dn_b.astype(jnp.float32))
        g = -jnp.exp(a_log[layer].astype(jnp.float32)) * jax.nn.softplus(
            (dn_a + dt_bias[layer]).astype(jnp.float32))
        o_dn = gated_delta_rule(dq, dk_, dvv, g, beta).astype(x.dtype)
        o_dn = rmsnorm(o_dn, dn_norm_w[layer]) * jax.nn.silu(dn_z.reshape(B, S, DN_HEADS, DN_HEAD_DIM))
        o_dn = o_dn.reshape(B, S, DN_WIDTH)

        aq = rope(at_q.reshape(B, S, AT_HEADS, AT_HEAD_DIM), positions)
        ak = rope(at_k.reshape(B, S, AT_HEADS, AT_HEAD_DIM), positions)
        av = at_v.reshape(B, S, AT_HEADS, AT_HEAD_DIM)
        o_at = dilated_attention(aq, ak, av).astype(x.dtype)
        o_at = rmsnorm(o_at, at_norm_w[layer]) * jax.nn.silu(at_z.reshape(B, S, AT_HEADS, AT_HEAD_DIM))
        o_at = o_at.reshape(B, S, AT_WIDTH)

        mix = jnp.concatenate([o_dn, o_at], axis=-1) @ w_out[layer]
        x = x + gate[:, None] * mix
    return rmsnorm(x, final_norm_w)
```

```python
import numpy as np
import concourse.bass as bass
import concourse.mybir as mybir
from concourse.bass_utils import run_bass_kernel_spmd

F32 = mybir.dt.float32
BF16 = mybir.dt.bfloat16
AF = mybir.ActivationFunctionType
ALU = mybir.AluOpType

NCORE = 8
S_TOT = 16384
D = 1024
TOK = S_TOT // NCORE
NT = TOK // 128
IN_COLS = 4104
EPS = 1e-6
PI = float(np.pi)


class Sched:
    def __init__(self, nc):
        self.nc = nc
        self.eng = {"pe": nc.tensor, "act": nc.scalar, "dve": nc.vector, "pool": nc.gpsimd, "sp": nc.sync}
        self.sem = {k: nc.alloc_semaphore("prog_" + k) for k in self.eng}
        self.cnt = {k: 0 for k in self.eng}
        self.waited = {k: {} for k in self.eng}
        self.dsem, self.dcnt, self.bufs = {}, {}, {}
        self.semh = {("e", k): self.sem[k] for k in self.eng}
        self.ns = ""
        self.shared = set()

    def _k(self, keys):
        if not self.ns:
            return list(keys)
        return [k if k in self.shared else k + self.ns for k in keys]

    def _deps(self, reads, writes):
        deps = []
        for b in reads:
            st = self.bufs.get(b)
            if st and st["w"]:
                deps.append(st["w"])
        for b in writes:
            st = self.bufs.get(b)
            if st:
                if st["w"]:
                    deps.append(st["w"])
                deps.extend(st["r"])
        return deps

    def _wait(self, e, deps):
        need = {}
        for (sk, v) in deps:
            if e == "pe" and sk == ("e", "pe"):
                continue
            if v > need.get(sk, 0):
                need[sk] = v
        for sk, v in need.items():
            if self.waited[e].get(sk, 0) >= v:
                continue
            self.eng[e].wait_ge(self.semh[sk], v)
            self.waited[e][sk] = v

    def _record(self, tag, reads, writes):
        for b in reads:
            st = self.bufs.setdefault(b, {"w": None, "r": []})
            st["r"].append(tag)
            if len(st["r"]) > 64:
                best = {}
                for (sk, v) in st["r"]:
                    best[sk] = max(best.get(sk, 0), v)
                st["r"] = list(best.items())
        for b in writes:
            self.bufs[b] = {"w": tag, "r": []}

    def op(self, e, fn, reads=(), writes=()):
        reads, writes = self._k(reads), self._k(writes)
        self._wait(e, self._deps(reads, writes))
        ins = fn()
        self.cnt[e] += 1
        ins.then_inc(self.sem[e], 1)
        self._record((("e", e), self.cnt[e]), reads, writes)
        return ins

    def dma(self, q, out, in_, reads=(), writes=(), key="d0"):
        reads, writes = self._k(reads), self._k(writes)
        key = q + "_" + key
        if key not in self.dsem:
            self.dsem[key] = self.nc.alloc_semaphore("dma_" + key)
            self.dcnt[key] = 0
            self.semh[("d", key)] = self.dsem[key]
        deps = self._deps(reads, writes)
        if self.dcnt[key] > 0:
            deps.append((("d", key), self.dcnt[key]))
        self._wait(q, deps)
        ins = self.eng[q].dma_start(out=out, in_=in_)
        self.dcnt[key] += 16
        ins.then_inc(self.dsem[key], 16)
        self._record((("d", key), self.dcnt[key]), reads, writes)
        return ins

    def wait_all(self, e):
        deps = [(("e", k), self.cnt[k]) for k in self.eng if self.cnt[k] > 0 and k != e]
        deps += [(("d", k), v) for k, v in self.dcnt.items()]
        self._wait(e, deps)


def make_ident(nc, S, name="ident"):
    identf = nc.alloc_sbuf_tensor(name + "f", [128, 128], F32)
    ident = nc.alloc_sbuf_tensor(name, [128, 128], BF16)
    S.op("pool", lambda: nc.gpsimd.memset(identf[:], 1.0), writes=[name + "f"])
    S.op("pool", lambda: nc.gpsimd.affine_select(out=identf[:], in_=identf[:], pattern=[[-1, 128]],
                                                  compare_op=ALU.is_equal, fill=0.0, base=0, channel_multiplier=1),
         reads=[name + "f"], writes=[name + "f"])
    S.op("pool", lambda: nc.gpsimd.tensor_copy(out=ident[:], in_=identf[:]), reads=[name + "f"], writes=[name])
    return ident, identf


def build_l1(ntiles=NT):
    nc = bass.Bass("TRN2", target_bir_lowering=False)
    x = nc.dram_tensor("x", [TOK, D], F32, kind="ExternalInput").ap()
    c_in = nc.dram_tensor("c", [128, 8], F32, kind="ExternalInput").ap()
    w_mod = nc.dram_tensor("w_mod", [D, 3 * D], F32, kind="ExternalInput").ap()
    b_mod = nc.dram_tensor("b_mod", [128, 24], F32, kind="ExternalInput").ap()
    norm_w = nc.dram_tensor("norm_w", [128, 8], F32, kind="ExternalInput").ap()
    w_in = nc.dram_tensor("w_in", [D, IN_COLS], F32, kind="ExternalInput").ap()
    proj = nc.dram_tensor("proj", [TOK, IN_COLS], F32, kind="ExternalOutput").ap()
    modT = nc.dram_tensor("modT", [128, 24], F32, kind="ExternalOutput").ap()
    pos_in = nc.dram_tensor("pos", [128, NT], mybir.dt.int32, kind="ExternalInput").ap()
    invf_in = nc.dram_tensor("invf", [128, 32], F32, kind="ExternalInput").ap()
    q_out = nc.dram_tensor("q_r", [TOK, 512], BF16, kind="ExternalOutput").ap()
    k_out = nc.dram_tensor("k_r", [TOK, 512], BF16, kind="ExternalOutput").ap()
    v_out = nc.dram_tensor("v_b", [TOK, 512], BF16, kind="ExternalOutput").ap()
    S = Sched(nc)
    A = nc.alloc_sbuf_tensor
    V, G, AC = nc.vector, nc.gpsimd, nc.scalar
    ident, identf = make_ident(nc, S)

    posi = A("posi", [128, NT], mybir.dt.int32); posf = A("posf", [128, NT], F32); invf = A("invf_sb", [128, 32], F32)
    A3 = [128, NT, 32]
    wst = [A("wst%d" % i, [128, 8, 512], F32) for i in range(2)]
    ang = A("ang", A3, F32); sn = A("sn", A3, F32); cs = A("cs", A3, F32)
    ki = A("ki", A3, mybir.dt.int32)
    tmpa, kf, xs_, stp = (wst[1][:, k_, :].rearrange("p (t i) -> p t i", t=NT) for k_ in range(4))
    S.dma("sp", posi[:], pos_in[:, :], writes=["posi"], key="c")
    S.dma("sp", invf[:], invf_in[:, :], writes=["invf"], key="c")
    S.op("dve", lambda: V.tensor_copy(out=posf[:], in_=posi[:]), reads=["posi"], writes=["posf"])
    S.op("dve", lambda: V.tensor_tensor(out=ang[:], in0=posf[:, :].unsqueeze(2).to_broadcast(A3),
                                        in1=invf[:, :].unsqueeze(1).to_broadcast(A3), op=ALU.mult), reads=["posf", "invf"], writes=["ang"])
    for dst, off, nm in ((sn, 0.0, "sn"), (cs, 0.5 * PI, "cs")):
        S.op("dve", lambda off=off: V.tensor_scalar(out=xs_, in0=ang[:], scalar1=off, scalar2=None, op0=ALU.add), reads=["ang"], writes=["wst1"])
        S.op("dve", lambda: V.tensor_scalar(out=tmpa, in0=xs_, scalar1=1.0 / (2 * PI), scalar2=None, op0=ALU.mult), reads=["wst1"], writes=["wst1"])
        S.op("dve", lambda: V.tensor_copy(out=ki[:], in_=tmpa), reads=["wst1"], writes=["ki"])
        S.op("dve", lambda: V.tensor_copy(out=kf, in_=ki[:]), reads=["ki"], writes=["wst1"])
        S.op("dve", lambda: V.scalar_tensor_tensor(out=tmpa, in0=kf, scalar=-2 * PI, in1=xs_, op0=ALU.mult, op1=ALU.add),
             reads=["wst1", "wst1"], writes=["wst1"])
        S.op("dve", lambda: V.tensor_scalar(out=stp, in0=tmpa, scalar1=-PI, scalar2=1e30, op0=ALU.add, op1=ALU.mult), reads=["wst1"], writes=["wst1"])
        S.op("dve", lambda: V.tensor_scalar(out=stp, in0=stp, scalar1=0.0, scalar2=1.0, op0=ALU.max, op1=ALU.min), reads=["wst1"], writes=["wst1"])
        S.op("dve", lambda: V.scalar_tensor_tensor(out=tmpa, in0=stp, scalar=-2 * PI, in1=tmpa, op0=ALU.mult, op1=ALU.add),
             reads=["wst1", "wst1"], writes=["wst1"])
        S.op("dve", lambda: V.tensor_scalar(out=tmpa, in0=tmpa, scalar1=-PI, scalar2=PI, op0=ALU.max, op1=ALU.min), reads=["wst1"], writes=["wst1"])
        S.op("act", lambda dst=dst: AC.activation(out=dst[:], in_=tmpa, func=AF.Sin), reads=["wst1"], writes=[nm])
    B4 = [128, 8, 2, 32]; B3 = [128, 8, 32]
    rt1 = A("rt1", B4, F32); rt2 = A("rt2", B4, F32)
    qo = [A("qo%d" % i, [128, 512], BF16) for i in range(2)]; ko = [A("ko%d" % i, [128, 512], BF16) for i in range(2)]
    vo = [A("vo%d" % i, [128, 512], BF16) for i in range(2)]

    ct = A("ct", [128, 8], F32)
    sc = A("sc", [128, 8], F32)
    bm = A("bm", [128, 24], F32)
    nw = A("nw", [128, 8], F32)
    md = A("md", [128, 24], F32)
    S.dma("sp", ct[:], c_in[:, :], writes=["ct"], key="c")
    S.dma("sp", bm[:], b_mod[:, :], writes=["bm"], key="c")
    S.dma("sp", nw[:], norm_w[:, :], writes=["nw"], key="c")
    S.op("act", lambda: nc.scalar.activation(out=sc[:], in_=ct[:], func=AF.Silu), reads=["ct"], writes=["sc"])
    wm = [A("wm%d" % i, [128, 8, 256], F32) for i in range(2)]
    pmod = nc.alloc_psum_tensor("pmod", [128, 24], F32)
    wmv = w_mod.rearrange("(kt p) m -> p kt m", p=128)
    for cg in range(12):
        b = cg % 2
        S.dma("sp" if b == 0 else "act", wm[b][:], wmv[:, :, cg * 256:(cg + 1) * 256], writes=["wm%d" % b], key="wm%d" % b)
        for j in range(2):
            mt = cg * 2 + j
            for kt in range(8):
                S.op("pe", lambda b=b, j=j, kt=kt, mt=mt: nc.tensor.matmul(
                    pmod[:, mt:mt + 1], lhsT=wm[b][:, kt, j * 128:(j + 1) * 128], rhs=sc[:, kt:kt + 1],
                    start=(kt == 0), stop=(kt == 7)), reads=["wm%d" % b, "sc"], writes=["pmod"])
    S.op("dve", lambda: nc.vector.tensor_tensor(out=md[:], in0=pmod[:], in1=bm[:], op=ALU.add),
         reads=["pmod", "bm"], writes=["md"])
    S.dma("sp", modT[:, :], md[:], reads=["md"], key="st")
    gcol = A("gcol", [128, 8], F32)
    sgf = A("sgf", [128, 8], F32)
    sgb = A("sgb", [128, 8], BF16)
    S.op("dve", lambda: nc.vector.scalar_tensor_tensor(out=gcol[:], in0=md[:, 8:16], scalar=1.0, in1=nw[:],
                                                       op0=ALU.add, op1=ALU.mult), reads=["md", "nw"], writes=["gcol"])
    S.op("dve", lambda: nc.vector.reciprocal(out=sgf[:], in_=gcol[:]), reads=["gcol"], writes=["sgf"])
    S.op("dve", lambda: nc.vector.tensor_tensor(out=sgb[:], in0=sgf[:], in1=md[:, 0:8], op=ALU.mult),
         reads=["sgf", "md"], writes=["sgb"])

    wb = A("wb", [128, 8, IN_COLS], BF16)
    wiv = w_in.rearrange("(kt p) m -> p kt m", p=128)
    CG = [(i * 512, 512) for i in range(8)] + [(4096, 8)]
    for ci, (c0, cw) in enumerate(CG):
        b = ci % 2
        S.dma("sp" if b == 0 else "act", wst[b][:, :, 0:cw], wiv[:, :, c0:c0 + cw], writes=["wst%d" % b], key="wi%d" % b)
        for kt in range(8):
            if kt % 2 == 0:
                S.op("dve", lambda b=b, kt=kt, c0=c0, cw=cw: nc.vector.tensor_scalar(
                    out=wb[:, kt, c0:c0 + cw], in0=wst[b][:, kt, 0:cw], scalar1=gcol[:, kt:kt + 1], scalar2=None,
                    op0=ALU.mult), reads=["wst%d" % b, "gcol"], writes=["wb_%d" % ci])
            else:
                S.op("act", lambda b=b, kt=kt, c0=c0, cw=cw: nc.scalar.activation(
                    out=wb[:, kt, c0:c0 + cw], in_=wst[b][:, kt, 0:cw], func=AF.Copy, scale=gcol[:, kt:kt + 1]),
                    reads=["wst%d" % b, "gcol"], writes=["wb_%d" % ci])
    ones1 = A("ones1", [1, 128], BF16)
    S.op("pool", lambda: nc.gpsimd.memset(ones1[:], 1.0), writes=["ones1"])
    brow = A("brow", [1, IN_COLS], BF16)
    biasb = A("biasb", [128, IN_COLS], F32)
    pb = nc.alloc_psum_tensor("pb", [128, 512], F32)
    for ci, (c0, cw) in enumerate(CG):
        for kt in range(8):
            S.op("pe", lambda kt=kt, c0=c0, cw=cw: nc.tensor.matmul(
                pb[0:1, 0:cw], lhsT=sgb[:, kt:kt + 1], rhs=wb[:, kt, c0:c0 + cw], start=(kt == 0), stop=(kt == 7)),
                reads=["sgb", "wb_%d" % ci], writes=["pb"])
        S.op("act", lambda c0=c0, cw=cw: nc.scalar.copy(out=brow[0:1, c0:c0 + cw], in_=pb[0:1, 0:cw]),
             reads=["pb"], writes=["brow"])
        S.op("pe", lambda c0=c0, cw=cw: nc.tensor.matmul(pb[:, 0:cw], lhsT=ones1[0:1, :], rhs=brow[0:1, c0:c0 + cw],
                                                         start=True, stop=True), reads=["ones1", "brow"], writes=["pb"])
        S.op("act", lambda c0=c0, cw=cw: nc.scalar.copy(out=biasb[:, c0:c0 + cw], in_=pb[:, 0:cw]),
             reads=["pb"], writes=["biasb"])

    xt = [A("xt%d" % i, [128, D], F32) for i in range(2)]
    junk = A("junk", [128, D], F32)
    xb = A("xb", [128, D], BF16)
    xT = [A("xT%d" % i, [128, 8, 128], BF16) for i in range(2)]
    ss = A("ss", [128, 1], F32)
    rs = A("rs", [128, 1], F32)
    ot = [A("ot%d" % i, [128, IN_COLS], F32) for i in range(2)]
    pT = nc.alloc_psum_tensor("pT", [128, 8, 128], BF16)
    pp = [nc.alloc_psum_tensor("pp%d" % i, [128, 512], F32) for i in range(2)]
    def stage_norm(t):
        b = t % 2
        S.dma("act", xt[b][:], x[t * 128:(t + 1) * 128, :], writes=["xt%d" % b], key="x%d" % b)
        S.op("act", lambda: nc.scalar.activation(out=junk[:], in_=xt[b][:], func=AF.Square, accum_out=ss[:]),
             reads=["xt%d" % b], writes=["junk", "ss"])
        S.op("dve", lambda: nc.vector.tensor_scalar(out=rs[:], in0=ss[:], scalar1=1.0 / D, scalar2=EPS,
                                                    op0=ALU.mult, op1=ALU.add), reads=["ss"], writes=["rs"])
        S.op("act", lambda: nc.scalar.activation(out=rs[:], in_=rs[:], func=AF.Sqrt), reads=["rs"], writes=["rs"])
        S.op("dve", lambda: nc.vector.reciprocal(out=rs[:], in_=rs[:]), reads=["rs"], writes=["rs"])
        S.op("act", lambda: nc.scalar.activation(out=xb[:], in_=xt[b][:], func=AF.Copy, scale=rs[:, 0:1]),
             reads=["xt%d" % b, "rs"], writes=["xb"])
        for kt in range(8):
            S.op("pe", lambda kt=kt: nc.tensor.transpose(out=pT[:, kt, :], in_=xb[:, kt * 128:(kt + 1) * 128],
                                                         identity=ident[:]), reads=["xb", "ident"], writes=["pT"])
        S.op("dve", lambda: nc.vector.tensor_copy(out=xT[b][:], in_=pT[:]), reads=["pT"], writes=["xT%d" % b])

    def stage_mm(t):
        b = t % 2
        for ci, (c0, cw) in enumerate(CG):
            pb_ = ci % 2
            for kt in range(8):
                S.op("pe", lambda kt=kt: nc.tensor.matmul(
                    pp[pb_][:, 0:cw], lhsT=xT[b][:, kt, :], rhs=wb[:, kt, c0:c0 + cw], start=(kt == 0), stop=(kt == 7)),
                    reads=["xT%d" % b, "wb_%d" % ci], writes=["pp%d" % pb_])
            S.op("dve", lambda: nc.vector.tensor_tensor(
                out=ot[b][:, c0:c0 + cw], in0=pp[pb_][:, 0:cw], in1=biasb[:, c0:c0 + cw], op=ALU.add),
                reads=["pp%d" % pb_, "biasb"], writes=["ot%d_%d" % (b, ci)])
            q_ = ("sp", "pool")[ci % 2]
            S.dma(q_, proj[t * 128:(t + 1) * 128, c0:c0 + cw], ot[b][:, c0:c0 + cw], reads=["ot%d_%d" % (b, ci)], key="st%d_%d" % (b, ci))
            if ci == 6:
                gk = ["ot%d_%d" % (b, g_) for g_ in (4, 5, 6)]
                for (c_lo, dst_t, dst_d, nm_) in ((2056, qo[b], q_out, "qo%d" % b), (2568, ko[b], k_out, "ko%d" % b)):
                    xv = ot[b][:, c_lo:c_lo + 512].rearrange("p (h two i) -> p h two i", h=8, two=2)
                    ov = dst_t[:, :].rearrange("p (h two i) -> p h two i", h=8, two=2)
                    cb = cs[:, t, :].unsqueeze(1).unsqueeze(1).to_broadcast(B4)
                    sb3 = sn[:, t, :].unsqueeze(1).to_broadcast(B3)
                    S.op("dve", lambda xv=xv, cb=cb: V.tensor_tensor(out=rt1[:], in0=xv, in1=cb, op=ALU.mult), reads=gk + ["cs"], writes=["rt1"])
                    S.op("pool", lambda xv=xv, sb3=sb3: G.tensor_tensor(out=rt2[:, :, 0, :], in0=xv[:, :, 1, :], in1=sb3, op=ALU.mult),
                         reads=gk + ["sn"], writes=["rt2a"])
                    S.op("pool", lambda xv=xv, sb3=sb3: G.tensor_tensor(out=rt2[:, :, 1, :], in0=xv[:, :, 0, :], in1=sb3, op=ALU.mult),
                         reads=gk + ["sn"], writes=["rt2b"])
                    S.op("dve", lambda ov=ov: V.tensor_tensor(out=ov[:, :, 0, :], in0=rt1[:, :, 0, :], in1=rt2[:, :, 0, :], op=ALU.subtract),
                         reads=["rt1", "rt2a"], writes=[nm_ + "a"])
                    S.op("dve", lambda ov=ov: V.tensor_tensor(out=ov[:, :, 1, :], in0=rt1[:, :, 1, :], in1=rt2[:, :, 1, :], op=ALU.add),
                         reads=["rt1", "rt2b"], writes=[nm_ + "b"])
                    S.dma("sp", dst_d[t * 128:(t + 1) * 128, :], dst_t[:], reads=[nm_ + "a", nm_ + "b"], key=nm_)
            if ci == 7:
                S.op("act", lambda: AC.copy(out=vo[b][:], in_=ot[b][:, 3080:3592]), reads=["ot%d_6" % b, "ot%d_7" % b], writes=["vo%d" % b])
                S.dma("sp", v_out[t * 128:(t + 1) * 128, :], vo[b][:], reads=["vo%d" % b], key="vo%d" % b)

    if ntiles > 0:
        stage_norm(0)
    for t in range(ntiles):
        if t + 1 < ntiles:
            stage_norm(t + 1)
        stage_mm(t)
    S.wait_all("sp")
    return nc


def run_l1(x, c, w_mod, b_mod, norm_w, w_in, positions=None):
    nc = build_l1()
    xs = np.ascontiguousarray(x.reshape(NCORE, TOK, D))
    common = {
        "c": np.ascontiguousarray(c.reshape(8, 128).T),
        "w_mod": np.ascontiguousarray(w_mod[0]),
        "b_mod": np.ascontiguousarray(b_mod.reshape(24, 128).T),
        "norm_w": np.ascontiguousarray(norm_w.reshape(8, 128).T),
        "w_in": np.ascontiguousarray(w_in[0]),
    }
    invf = (10000.0 ** (-np.arange(32, dtype=np.float32) / 32)).astype(np.float32)
    common["invf"] = np.ascontiguousarray(np.broadcast_to(invf[None, :], (128, 32)))
    if positions is None:
        positions = np.arange(S_TOT, dtype=np.int32)[None]
    pos = np.asarray(positions).reshape(NCORE, NT, 128)
    res = run_bass_kernel_spmd(nc, [dict(common, x=xs[i], pos=np.ascontiguousarray(pos[i].T.astype(np.int32))) for i in range(NCORE)],
                               core_ids=list(range(NCORE)))
    proj = np.concatenate([r["proj"] for r in res.results], axis=0)
    modT = res.results[0]["modT"]
    qkv = tuple(np.concatenate([r[k] for r in res.results], axis=0) for k in ("q_r", "k_r", "v_b"))
    return proj, modT, qkv


AX = mybir.AxisListType.X


def build_dn(phase):
    nc = bass.Bass("TRN2", target_bir_lowering=False)
    if phase == 1:
        qkv_in = nc.dram_tensor("qkv_fm", [12, 128, TOK + 3], F32, kind="ExternalInput").ap()
        convw_in = nc.dram_tensor("convw", [128, 12, 4], F32, kind="ExternalInput").ap()
        ba_in = nc.dram_tensor("ba", [128, NT, 8], F32, kind="ExternalInput").ap()
        alog_in = nc.dram_tensor("alog", [128, 4], F32, kind="ExternalInput").ap()
        dtb_in = nc.dram_tensor("dtb", [128, 4], F32, kind="ExternalInput").ap()
        st_out = nc.dram_tensor("st_out", [128, 4, 256], F32, kind="ExternalOutput").ap()
        prod_out = nc.dram_tensor("prod", [NT, 6, 128, 512], BF16, kind="ExternalOutput").ap()
        sc_out = nc.dram_tensor("scal", [NT, 128, 8], F32, kind="ExternalOutput").ap()
    else:
        prod_in = nc.dram_tensor("prod", [NT, 6, 128, 512], BF16, kind="ExternalInput").ap()
        sc_in = nc.dram_tensor("scal", [NT, 128, 8], F32, kind="ExternalInput").ap()
        z_in = nc.dram_tensor("z", [TOK, 512], F32, kind="ExternalInput").ap()
        tt_in = nc.dram_tensor("tt_all", [8, 128, 4, 128], F32, kind="ExternalInput").ap()
        bb_in = nc.dram_tensor("bb_all", [8, 128, 4, 128], F32, kind="ExternalInput").ap()
        fl_in = nc.dram_tensor("flags", [128, 8], F32, kind="ExternalInput").ap()
        o_out = nc.dram_tensor("o_dn", [TOK, 512], F32, kind="ExternalOutput").ap()
    S = Sched(nc)
    A = nc.alloc_sbuf_tensor
    V, G, AC, PE = nc.vector, nc.gpsimd, nc.scalar, nc.tensor
    ident, identf = make_ident(nc, S)
    H4 = [128, 4, 128]

    def bc_h(ap2):
        return ap2.unsqueeze(1).to_broadcast(H4)

    def bc_c(ap2):
        return ap2.unsqueeze(2).to_broadcast(H4)

    if phase == 1:
        ltri = A("ltri", [128, 128], F32)
        strict = A("strict", [128, 128], F32)
        tril = A("tril", [128, 128], F32)
        onesf = A("onesf", [128, 128], F32)
        for nm, tl, pat, cm, cop in (("ltri", ltri, [[1, 128]], -1, ALU.is_ge), ("strict", strict, [[-1, 128]], 1, ALU.is_gt),
                                     ("tril", tril, [[-1, 128]], 1, ALU.is_ge)):
            S.op("pool", lambda tl=tl: G.memset(tl[:], 1.0), writes=[nm])
            S.op("pool", lambda tl=tl, pat=pat, cm=cm, cop=cop: G.affine_select(
                out=tl[:], in_=tl[:], pattern=pat, compare_op=cop, fill=0.0, base=0, channel_multiplier=cm),
                reads=[nm], writes=[nm])
        S.op("pool", lambda: G.memset(onesf[:], 1.0), writes=["onesf"])

        cw = A("cw", [128, 12, 4], F32)
        S.dma("sp", cw[:], convw_in[:, :, :], writes=["cw"], key="c")
        qkvT = A("qkvT", [128, 12, TOK], BF16)
        cin = [A("cin%d" % i, [128, TOK + 3], F32) for i in range(2)]
        cacc = [A("cacc0", [128, TOK], F32)] * 2
        for ct in range(12):
            b = ct % 2
            e = "dve"
            eng = V
            S.dma("sp" if b == 0 else "act", cin[b][:], qkv_in[ct, :, :], writes=["cin%d" % b], key="ci%d" % b)
            S.op(e, lambda eng=eng, b=b, ct=ct: eng.tensor_scalar(out=cacc[b][:], in0=cin[b][:, 0:TOK], scalar1=cw[:, ct, 0:1],
                                                                scalar2=None, op0=ALU.mult),
                 reads=["cin%d" % b, "cw"], writes=["cacc0"])
            for j in range(1, 4):
                S.op(e, lambda eng=eng, b=b, ct=ct, j=j: eng.scalar_tensor_tensor(
                    out=cacc[b][:], in0=cin[b][:, j:j + TOK], scalar=cw[:, ct, j:j + 1], in1=cacc[b][:],
                    op0=ALU.mult, op1=ALU.add), reads=["cin%d" % b, "cw", "cacc0"], writes=["cacc0"])
            S.op("act", lambda b=b, ct=ct: AC.activation(out=qkvT[:, ct, :], in_=cacc[b][:], func=AF.Silu),
                 reads=["cacc0"], writes=["qkvT%d" % ct])

        ba = A("ba_sb", [128, NT, 8], F32)
        al = A("al", [128, 4], F32)
        db = A("db", [128, 4], F32)
        S.dma("sp", ba[:], ba_in[:, :, :], writes=["ba"], key="c")
        S.dma("sp", al[:], alog_in[:, :], writes=["al"], key="c")
        S.dma("sp", db[:], dtb_in[:, :], writes=["db"], key="c")
        G3 = [128, NT, 4]
        xa = A("xa", G3, F32); axa = A("axa", G3, F32); rl = A("rl", G3, F32)
        gg = A("gg", G3, F32); beta = A("beta", G3, F32); nbeta = A("nbeta", G3, F32); ea = A("ea", [128, 4], F32)
        S.op("dve", lambda: V.tensor_tensor(out=xa[:], in0=ba[:, :, 4:8], in1=db[:, :].unsqueeze(1).to_broadcast(G3), op=ALU.add),
             reads=["ba", "db"], writes=["xa"])
        S.op("act", lambda: AC.activation(out=axa[:], in_=xa[:], func=AF.Abs), reads=["xa"], writes=["axa"])
        S.op("act", lambda: AC.activation(out=axa[:], in_=axa[:], func=AF.Exp, scale=-1.0), reads=["axa"], writes=["axa"])
        S.op("dve", lambda: V.tensor_scalar(out=axa[:], in0=axa[:], scalar1=1.0, scalar2=None, op0=ALU.add), reads=["axa"], writes=["axa"])
        S.op("act", lambda: AC.activation(out=axa[:], in_=axa[:], func=AF.Ln), reads=["axa"], writes=["axa"])
        S.op("dve", lambda: V.tensor_single_scalar(out=rl[:], in_=xa[:], scalar=0.0, op=ALU.max), reads=["xa"], writes=["rl"])
        S.op("dve", lambda: V.tensor_tensor(out=rl[:], in0=rl[:], in1=axa[:], op=ALU.add), reads=["rl", "axa"], writes=["rl"])
        S.op("act", lambda: AC.activation(out=ea[:], in_=al[:], func=AF.Exp), reads=["al"], writes=["ea"])
        S.op("dve", lambda: V.scalar_tensor_tensor(out=gg[:], in0=rl[:], scalar=-1.0, in1=ea[:, :].unsqueeze(1).to_broadcast(G3),
                                                   op0=ALU.mult, op1=ALU.mult), reads=["rl", "ea"], writes=["gg"])
        S.op("act", lambda: AC.activation(out=beta[:], in_=ba[:, :, 0:4], func=AF.Sigmoid), reads=["ba"], writes=["beta"])
        S.op("dve", lambda: V.tensor_scalar(out=nbeta[:], in0=beta[:], scalar1=-1.0, scalar2=None, op0=ALU.mult),
             reads=["beta"], writes=["nbeta"])

    pf = [nc.alloc_psum_tensor("pf%d" % i, [128, 512], F32) for i in range(4)]
    pbf = [nc.alloc_psum_tensor("pbf%d" % i, [128, 1024], BF16) for i in range(2)]
    pS = nc.alloc_psum_tensor("pS", [128, 4, 256], F32)
    rr = {"pf": 0, "pbf": 0}

    def npf():
        i = rr["pf"]; rr["pf"] = (i + 1) % 4
        return pf[i], "pf%d" % i

    def npbf():
        i = rr["pbf"]; rr["pbf"] = (i + 1) % 2
        return pbf[i], "pbf%d" % i

    def v4(p):
        return p[:, 0:512].rearrange("p (h c) -> p h c", h=4)

    def mm4(lhs, lk, rhs, rk, n=128, rhs_shared=False, lhs_shared=False):
        p, pk = npf()
        for h in range(4):
            l = lhs if lhs_shared else lhs(h)
            r = rhs if rhs_shared else rhs(h)
            S.op("pe", lambda l=l, r=r, h=h: PE.matmul(p[:, h * 128:h * 128 + n], lhsT=l, rhs=r, start=True, stop=True),
                 reads=[lk, rk], writes=[pk])
        return p, pk

    def tr4(src, sk):
        p, pk = npbf()
        for h in range(4):
            S.op("pe", lambda h=h: PE.transpose(out=p[:, h * 128:(h + 1) * 128], in_=src(h), identity=ident[:]),
                 reads=[sk, "ident"], writes=[pk])
        return p[:, 0:512].rearrange("p (h c) -> p h c", h=4), pk

    def tr4f(src, sk):
        p, pk = npf()
        for h in range(4):
            S.op("pe", lambda h=h: PE.transpose(out=p[:, h * 128:(h + 1) * 128], in_=src(h), identity=identf[:]),
                 reads=[sk, "identf"], writes=[pk])
        return v4(p), pk

    NAMES = ("G1", "gB", "dec", "decT", "decS", "Ee", "qd", "sq", "kn", "knT", "vb", "Pb", "Qb", "Rb", "Xb", "kbg", "kd", "Wm", "WT",
             "nMT", "attn", "attnT", "UA", "scq", "rq", "gl", "gcs", "egc", "ekd", "bege", "ssk", "rk", "ssq")

    def alloc_set(x):
        def T(name, dt=BF16, shape=H4):
            return A(name + x, shape, dt)
        L = {}
        for nm in ("G1", "gB", "dec", "decT", "decS", "Ee", "sq"):
            L[nm] = T(nm, F32)
        for nm in ("qd", "kn", "knT", "vb", "Xb", "kbg", "kd", "Wm", "WT", "nMT", "attn", "attnT"):
            L[nm] = T(nm)
        L["Pb"] = [T("P0", F32), T("P1", F32)]; L["Qb"] = [T("Q0", F32), T("Q1", F32)]; L["Rb"] = [T("R0", F32), T("R1", F32)]
        L["UA"] = A("UA" + x, [128, 4, 256], BF16)
        L["scq"] = A("scq" + x, [128, 8], F32); L["rq"] = L["scq"][:, 0:4]; L["gl"] = L["scq"][:, 4:8]
        L["gcs"] = A("gcs" + x, [128, 8], F32)
        for nm in ("egc", "ekd", "bege", "ssk", "rk", "ssq"):
            L[nm] = A(nm + x, [128, 4], F32)
        return L
    if phase == 1:
        SETS = [alloc_set("_a"), alloc_set("_b")]
    else:
        vnew = A("vnew", H4, BF16); osb = A("osb", H4, F32); yb = A("yb", H4, F32); sq = A("sq", H4, F32)
        sso = A("sso", [128, 4], F32); ro = A("ro", [128, 4], F32)
    STf = A("STf", [128, 4, 256], F32); STb = A("STb", [128, 4, 256], BF16)
    if phase == 1:
        for L_, x_ in zip(SETS, ("_a", "_b")):
            S.op("pool", lambda L_=L_: G.memset(L_["UA"][:], 0.0), writes=["UA" + x_])
    S.op("pool", lambda: G.memset(STf[:], 0.0), writes=["STf"])
    Wd = 256 if phase == 1 else 128

    if phase == 1:
        for h in range(4):
            S.op("pool", lambda h=h: G.tensor_copy(out=STf[:, h, 128:256], in_=identf[:]), reads=["identf", "STf"], writes=["STf"])
    else:
        fl = A("fl", [128, 8], F32)
        S.dma("sp", fl[:], fl_in[:, :], writes=["fl"], key="c")
        ttf = A("ttf", H4, F32); ttb = A("ttb", H4, BF16); bbf = A("bbf", H4, F32); tmpc = A("tmpc", H4, F32)
        S.op("act", lambda: AC.copy(out=STb[:], in_=STf[:]), reads=["STf"], writes=["STb"])
        for j in range(7):
            S.dma("sp", ttf[:], tt_in[j, :, :, :], writes=["ttf"], key="cc")
            S.dma("act", bbf[:], bb_in[j, :, :, :], writes=["bbf"], key="cc2")
            S.op("dve", lambda: V.tensor_copy(out=ttb[:], in_=ttf[:]), reads=["ttf"], writes=["ttb"])
            p, pk = mm4(lambda h: ttb[:, h, :], "ttb", lambda h: STb[:, h, 0:128], "STb")
            S.op("dve", lambda p=p: V.tensor_tensor(out=tmpc[:], in0=v4(p), in1=bbf[:], op=ALU.add), reads=[pk, "bbf"], writes=["tmpc"])
            S.op("dve", lambda: V.tensor_tensor(out=tmpc[:], in0=tmpc[:], in1=STf[:, :, 0:128], op=ALU.subtract),
                 reads=["tmpc", "STf"], writes=["tmpc"])
            S.op("dve", lambda j=j: V.scalar_tensor_tensor(out=STf[:, :, 0:128], in0=tmpc[:], scalar=fl[:, j:j + 1], in1=STf[:, :, 0:128],
                                                         op0=ALU.mult, op1=ALU.add), reads=["tmpc", "fl", "STf"], writes=["STf"])
            S.op("act", lambda: AC.copy(out=STb[:], in_=STf[:]), reads=["STf"], writes=["STb"])
        zt = A("zt", [128, 512], F32); sz = A("sz", [128, 512], F32)
        PR = [A("PR%d" % i, [128, 6, 512], BF16) for i in range(2)]; SCL = [A("SCL%d" % i, [128, 8], F32) for i in range(2)]

        def load_prod(t):
            b = t % 2
            S.dma("sp" if b == 0 else "act", PR[b][:], prod_in[t].rearrange("k p f -> p k f"), writes=["PR%d" % b], key="pr%d" % b)
            S.dma("sp", SCL[b][:], sc_in[t], writes=["SCL%d" % b], key="sc%d" % b)
        load_prod(0)
    if phase == 1:
        S.op("act", lambda: AC.copy(out=STb[:], in_=STf[:]), reads=["STf"], writes=["STb"])

    QK_ALL = ["qkvT%d" % i for i in range(12)]

    def tile_gen(t, L):
        (G1, gB, dec, decT, decS, Ee, qd, sq, kn, knT, vb, Pb, Qb, Rb, Xb, kbg, kd, Wm, WT,
         nMT, attn, attnT, UA, scq, rq, gl, gcs, egc, ekd, bege, ssk, rk, ssq) = (L[k] for k in NAMES)
        ts = slice(t * 128, (t + 1) * 128)
        g_t = gg[:, t, :]
        S.op("pool", lambda: G.tensor_tensor(out=G1[:], in0=bc_h(ltri[:, :]), in1=bc_c(g_t), op=ALU.mult), reads=["ltri", "gg"], writes=["G1"])
        S.op("pool", lambda: G.tensor_tensor(out=gB[:], in0=bc_h(onesf[:, :]), in1=bc_c(g_t), op=ALU.mult), reads=["onesf", "gg"], writes=["gB"])
        yield
        p, pk = mm4(lambda h: G1[:, h, :], "G1", strict[:, :], "strict", rhs_shared=True)
        S.op("act", lambda p=p: AC.activation(out=dec[:], in_=v4(p), func=AF.Exp), reads=[pk], writes=["dec"])
        S.op("pool", lambda: G.tensor_tensor(out=decT[:], in0=dec[:], in1=bc_h(tril[:, :]), op=ALU.mult), reads=["dec", "tril"], writes=["decT"])
        S.op("pool", lambda: G.tensor_tensor(out=decS[:], in0=dec[:], in1=bc_h(strict[:, :]), op=ALU.mult), reads=["dec", "strict"], writes=["decS"])
        yield
        p, pk = npf()
        S.op("pe", lambda p=p: PE.matmul(p[:, 0:4], lhsT=ltri[:, :], rhs=g_t, start=True, stop=True), reads=["ltri", "gg"], writes=[pk])
        S.op("pe", lambda p=p: PE.matmul(p[:, 4:8], lhsT=onesf[:, :], rhs=g_t, start=True, stop=True), reads=["onesf", "gg"], writes=[pk])
        S.op("dve", lambda p=p: V.tensor_copy(out=gcs[:], in_=p[:, 0:8]), reads=[pk], writes=["gcs"])
        S.op("act", lambda: AC.activation(out=egc[:], in_=gcs[:, 0:4], func=AF.Exp), reads=["gcs"], writes=["egc"])
        S.op("act", lambda: AC.activation(out=gl, in_=gcs[:, 4:8], func=AF.Exp), reads=["gcs"], writes=["gl"])
        S.op("dve", lambda: V.tensor_tensor(out=ekd[:], in0=gcs[:, 4:8], in1=gcs[:, 0:4], op=ALU.subtract), reads=["gcs"], writes=["ekd"])
        S.op("act", lambda: AC.activation(out=ekd[:], in_=ekd[:], func=AF.Exp), reads=["ekd"], writes=["ekd"])
        S.op("dve", lambda: V.tensor_tensor(out=bege[:], in0=beta[:, t, :], in1=egc[:], op=ALU.mult), reads=["beta", "egc"], writes=["bege"])
        yield
        p, pk = mm4(lambda h: gB[:, h, :], "gB", ltri[:, :], "ltri", rhs_shared=True)
        S.op("act", lambda p=p: AC.activation(out=Ee[:], in_=v4(p), func=AF.Exp), reads=[pk], writes=["Ee"])
        S.op("dve", lambda: V.tensor_tensor(out=qd[:], in0=qkvT[:, 0:4, ts], in1=Ee[:], op=ALU.mult), reads=QK_ALL[0:4] + ["Ee"], writes=["qd"])
        yield
        pv_, pk = tr4(lambda h: qkvT[:, 4 + h, ts], "qkvT4")
        for kk in QK_ALL[4:8]:
            S.bufs.setdefault(kk, {"w": None, "r": []})
        S.op("act", lambda pv_=pv_: AC.activation(out=sq[:], in_=pv_, func=AF.Square), reads=[pk] + QK_ALL[4:8], writes=["sq"])
        S.op("dve", lambda: V.tensor_reduce(out=ssk[:], in_=sq[:], axis=AX, op=ALU.add), reads=["sq"], writes=["ssk"])
        S.op("dve", lambda: V.tensor_scalar(out=ssk[:], in0=ssk[:], scalar1=EPS, scalar2=None, op0=ALU.add), reads=["ssk"], writes=["ssk"])
        S.op("act", lambda: AC.activation(out=ssk[:], in_=ssk[:], func=AF.Sqrt), reads=["ssk"], writes=["ssk"])
        S.op("dve", lambda: V.reciprocal(out=rk[:], in_=ssk[:]), reads=["ssk"], writes=["rk"])
        S.op("dve", lambda pv_=pv_: V.tensor_tensor(out=kn[:], in0=pv_, in1=bc_c(rk[:, :]), op=ALU.mult), reads=[pk, "rk"], writes=["kn"])
        yield
        pv_, pk = tr4(lambda h: kn[:, h, :], "kn")
        S.op("act", lambda pv_=pv_: AC.copy(out=knT[:], in_=pv_), reads=[pk], writes=["knT"])
        yield
        pv_, pk = tr4(lambda h: qkvT[:, 8 + h, ts], "qkvT8")
        S.op("dve", lambda pv_=pv_: V.tensor_tensor(out=vb[:], in0=pv_, in1=bc_c(beta[:, t, :]), op=ALU.mult),
             reads=[pk, "beta"] + QK_ALL[8:12], writes=["vb"])
        yield
        pv_, pk = tr4(lambda h: qkvT[:, h, ts], "qkvT0")
        S.op("act", lambda pv_=pv_: AC.activation(out=sq[:], in_=pv_, func=AF.Square), reads=[pk] + QK_ALL[0:4], writes=["sq"])
        S.op("dve", lambda: V.tensor_reduce(out=ssq[:], in_=sq[:], axis=AX, op=ALU.add), reads=["sq"], writes=["ssq"])
        S.op("dve", lambda: V.tensor_scalar(out=ssq[:], in0=ssq[:], scalar1=EPS, scalar2=128.0, op0=ALU.add, op1=ALU.mult),
             reads=["ssq"], writes=["ssq"])
        S.op("act", lambda: AC.activation(out=ssq[:], in_=ssq[:], func=AF.Sqrt), reads=["ssq"], writes=["ssq"])
        S.op("dve", lambda: V.reciprocal(out=rq, in_=ssq[:]), reads=["ssq"], writes=["rq"])
        yield
        p, pk = mm4(lambda h: knT[:, h, :], "knT", lambda h: knT[:, h, :], "knT")
        for h in range(4):
            S.op("dve", lambda p=p, h=h: V.scalar_tensor_tensor(out=Pb[0][:, h, :], in0=p[:, h * 128:(h + 1) * 128],
                                                               scalar=nbeta[:, t, h:h + 1], in1=decS[:, h, :], op0=ALU.mult, op1=ALU.mult),
                 reads=[pk, "nbeta", "decS"], writes=["P0"])
        yield
        pv_, pk = tr4f(lambda h: Pb[0][:, h, :], "P0")
        S.op("act", lambda pv_=pv_: AC.copy(out=Qb[0][:], in_=pv_), reads=[pk], writes=["Q0"])
        S.op("pool", lambda: G.tensor_tensor(out=Rb[0][:], in0=Qb[0][:], in1=bc_h(identf[:, :]), op=ALU.add), reads=["Q0", "identf"], writes=["R0"])
        cp, cq, cr = 0, 0, 0
        for k in range(6):
            np_, nq, nr = 1 - cp, 1 - cq, 1 - cr
            yield
            p1, pk1 = mm4(lambda h: Qb[cq][:, h, :], "Q%d" % cq, lambda h: Pb[cp][:, h, :], "P%d" % cp)
            if k < 5:
                yield
                p2, pk2 = mm4(lambda h: Pb[cp][:, h, :], "P%d" % cp, lambda h: Qb[cq][:, h, :], "Q%d" % cq)
            S.op("act", lambda p1=p1, np_=np_: AC.copy(out=Pb[np_][:], in_=v4(p1)), reads=[pk1], writes=["P%d" % np_])
            if k < 5:
                S.op("dve", lambda p2=p2, nq=nq: V.tensor_copy(out=Qb[nq][:], in_=v4(p2)), reads=[pk2], writes=["Q%d" % nq])
            yield
            p3, pk3 = mm4(lambda h: Pb[np_][:, h, :], "P%d" % np_, lambda h: Rb[cr][:, h, :], "R%d" % cr)
            S.op("dve", lambda p3=p3, nr=nr, cr=cr: V.tensor_tensor(out=Rb[nr][:], in0=v4(p3), in1=Rb[cr][:], op=ALU.add),
                 reads=[pk3, "R%d" % cr], writes=["R%d" % nr])
            cp, cq, cr = np_, nq, nr
        S.op("act", lambda cr=cr: AC.copy(out=Xb[:], in_=Rb[cr][:]), reads=["R%d" % cr], writes=["Xb"])
        X = Xb; xk = "Xb"
        S.op("pool", lambda: G.tensor_tensor(out=kbg[:], in0=kn[:], in1=bc_c(bege[:, :]), op=ALU.mult), reads=["kn", "bege"], writes=["kbg"])
        S.op("pool", lambda: G.tensor_tensor(out=kd[:], in0=kn[:], in1=bc_c(ekd[:, :]), op=ALU.mult), reads=["kn", "ekd"], writes=["kd"])
        yield
        p, pk = mm4(lambda h: X[:, h, :], xk, lambda h: kbg[:, h, :], "kbg")
        S.op("act", lambda p=p: AC.copy(out=Wm[:], in_=v4(p)), reads=[pk], writes=["Wm"])
        yield
        p, pk = mm4(lambda h: X[:, h, :], xk, lambda h: vb[:, h, :], "vb")
        S.op("dve", lambda p=p: V.tensor_copy(out=UA[:, :, 0:128], in_=v4(p)), reads=[pk], writes=["UA"])
        yield
        p, pk = mm4(lambda h: Wm[:, h, :], "Wm", lambda h: kd[:, h, :], "kd")
        S.op("act", lambda p=p: AC.mul(out=nMT[:], in_=v4(p), mul=-1.0), reads=[pk], writes=["nMT"])
        yield
        p, pk = mm4(lambda h: kbg[:, h, :], "kbg", lambda h: X[:, h, :], xk)
        S.op("act", lambda p=p: AC.copy(out=WT[:], in_=v4(p)), reads=[pk], writes=["WT"])
        yield
        p, pk = mm4(lambda h: qkvT[:, h, ts], "qkvT0", lambda h: knT[:, h, :], "knT")
        S.op("dve", lambda p=p: V.tensor_tensor(out=attn[:], in0=v4(p), in1=decT[:], op=ALU.mult), reads=[pk, "decT"] + QK_ALL[0:4], writes=["attn"])
        yield
        pv_, pk = tr4(lambda h: attn[:, h, :], "attn")
        S.op("act", lambda pv_=pv_: AC.copy(out=attnT[:], in_=pv_), reads=[pk], writes=["attnT"])
        for kx, (tl, tk) in enumerate(((WT, "WT"), (None, "UA"), (qd, "qd"), (attnT, "attnT"), (nMT, "nMT"), (kd, "kd"))):
            src = UA[:, :, 0:128] if tl is None else tl[:]
            S.dma("pool", prod_out[t, kx].rearrange("p (h c) -> p h c", h=4), src, reads=[tk], key="sp%d" % kx)
        S.dma("pool", sc_out[t], scq[:], reads=["rq", "gl"], key="spsc")
        WTv, Uv, qdv, attnTv, nMTv, kdv, rqv, glv = WT, UA, qd, attnT, nMT, kd, rq, gl
        kWT, kU, kqd, kat, knm, kkd, krq, kgl = "WT", "UA", "qd", "attnT", "nMT", "kd", "rq", "gl"
        yield
        for h in range(4):
            S.op("pe", lambda h=h: PE.matmul(pS[:, h, 0:Wd], lhsT=kdv[:, h, :], rhs=Uv[:, h, 0:Wd], start=True, stop=False),
                 reads=[kkd, kU], writes=["pS"])
            S.op("pe", lambda h=h: PE.matmul(pS[:, h, 0:Wd], lhsT=nMTv[:, h, :], rhs=STb[:, h, 0:Wd], start=False, stop=True),
                 reads=[knm, "STb"], writes=["pS"])
        for h in range(4):
            S.op("dve", lambda h=h: V.scalar_tensor_tensor(out=STf[:, h, 0:Wd], in0=STf[:, h, 0:Wd], scalar=glv[:, h:h + 1], in1=pS[:, h, 0:Wd],
                                                         op0=ALU.mult, op1=ALU.add), reads=["STf", kgl, "pS"], writes=["STf"])
        S.op("act", lambda: AC.copy(out=STb[:, :, 0:Wd], in_=STf[:, :, 0:Wd]), reads=["STf"], writes=["STb"])

    if phase == 1:
        S.shared = set(S.bufs.keys()) | {"pf%d" % i for i in range(4)} | {"pbf0", "pbf1", "pS", "STf", "STb"}
        for t0 in range(0, NT, 2):
            gens = [(tile_gen(t0, SETS[0]), "_a"), (tile_gen(t0 + 1, SETS[1]), "_b")]
            while gens:
                for g_, ns_ in list(gens):
                    S.ns = ns_
                    try:
                        next(g_)
                    except StopIteration:
                        gens.remove((g_, ns_))
            S.ns = ""
    if phase == 2:
        STb3 = [STb, A("STb1", [128, 4, 256], BF16), A("STb2", [128, 4, 256], BF16)]
        kST = ["STb", "STb1", "STb2"]

        def prods(t):
            b = t % 2
            pv6 = PR[b][:, :, :].rearrange("p k (h c) -> p k h c", h=4)
            return tuple(pv6[:, kx] for kx in range(6)) + (SCL[b][:, 0:4], SCL[b][:, 4:8], "PR%d" % b, "SCL%d" % b)

        def upd(t):
            WTv, Uv, qdv, attnTv, nMTv, kdv, rqv, glv, kP, kS = prods(t)
            cur, nxt = STb3[t % 3], STb3[(t + 1) % 3]
            kc, kn_ = kST[t % 3], kST[(t + 1) % 3]
            for h in range(4):
                S.op("pe", lambda h=h: PE.matmul(pS[:, h, 0:Wd], lhsT=kdv[:, h, :], rhs=Uv[:, h, 0:Wd], start=True, stop=False),
                     reads=[kP], writes=["pS"])
                S.op("pe", lambda h=h: PE.matmul(pS[:, h, 0:Wd], lhsT=nMTv[:, h, :], rhs=cur[:, h, 0:Wd], start=False, stop=True),
                     reads=[kP, kc], writes=["pS"])
            for h in range(4):
                S.op("dve", lambda h=h: V.scalar_tensor_tensor(out=STf[:, h, 0:Wd], in0=STf[:, h, 0:Wd], scalar=glv[:, h:h + 1], in1=pS[:, h, 0:Wd],
                                                             op0=ALU.mult, op1=ALU.add), reads=["STf", kS, "pS"], writes=["STf"])
            S.op("act", lambda: AC.copy(out=nxt[:, :, 0:Wd], in_=STf[:, :, 0:Wd]), reads=["STf"], writes=[kn_])

        def out(t):
            ts = slice(t * 128, (t + 1) * 128)
            WTv, Uv, qdv, attnTv, nMTv, kdv, rqv, glv, kP, kS = prods(t)
            cur, kc = STb3[t % 3], kST[t % 3]
            p, pk = mm4(lambda h: WTv[:, h, :], kP, lambda h: cur[:, h, 0:128], kc)
            S.op("dve", lambda p=p: V.tensor_tensor(out=vnew[:], in0=Uv[:, :, 0:128], in1=v4(p), op=ALU.subtract), reads=[pk, kP], writes=["vnew"])
            p, pk = npf()
            for h in range(4):
                S.op("pe", lambda p=p, h=h: PE.matmul(p[:, h * 128:(h + 1) * 128], lhsT=qdv[:, h, :], rhs=cur[:, h, 0:128], start=True, stop=False),
                     reads=[kP, kc], writes=[pk])
                S.op("pe", lambda p=p, h=h: PE.matmul(p[:, h * 128:(h + 1) * 128], lhsT=attnTv[:, h, :], rhs=vnew[:, h, :], start=False, stop=True),
                     reads=[kP, "vnew"], writes=[pk])
            S.op("dve", lambda p=p: V.tensor_tensor(out=osb[:], in0=v4(p), in1=bc_c(rqv), op=ALU.mult), reads=[pk, kS], writes=["osb"])
            S.op("act", lambda: AC.activation(out=sq[:], in_=osb[:], func=AF.Square), reads=["osb"], writes=["sq"])
            S.op("dve", lambda: V.tensor_reduce(out=sso[:], in_=sq[:], axis=AX, op=ALU.add), reads=["sq"], writes=["sso"])
            S.op("dve", lambda: V.tensor_scalar(out=sso[:], in0=sso[:], scalar1=1.0 / 128, scalar2=EPS, op0=ALU.mult, op1=ALU.add),
                 reads=["sso"], writes=["sso"])
            S.op("act", lambda: AC.activation(out=sso[:], in_=sso[:], func=AF.Sqrt), reads=["sso"], writes=["sso"])
            S.op("dve", lambda: V.reciprocal(out=ro[:], in_=sso[:]), reads=["sso"], writes=["ro"])
            S.dma("sp", zt[:], z_in[ts, :], writes=["zt"], key="z")
            S.op("act", lambda: AC.activation(out=sz[:], in_=zt[:], func=AF.Silu), reads=["zt"], writes=["sz"])
            S.op("dve", lambda: V.tensor_tensor(out=yb[:], in0=osb[:], in1=bc_c(ro[:, :]), op=ALU.mult), reads=["osb", "ro"], writes=["yb"])
            S.op("pool", lambda: G.tensor_tensor(out=yb[:], in0=yb[:], in1=sz[:, :].rearrange("p (h c) -> p h c", h=4), op=ALU.mult),
                 reads=["yb", "sz"], writes=["yb"])
            S.dma("pool", o_out[ts, :], yb[:].rearrange("p h c -> p (h c)"), reads=["yb"], key="st")

        if NT > 1:
            upd(0)
        for t in range(NT):
            if t + 1 < NT:
                load_prod(t + 1)
                if t + 2 < NT:
                    upd(t + 1)
            out(t)
    if phase == 1:
        S.dma("sp", st_out[:, :, :], STf[:], reads=["STf"], key="st")
    S.wait_all("sp")
    return nc


def dn_inputs(proj, conv_w, a_log, dt_bias):
    qkv = proj[:, 0:1536]
    pad = np.concatenate([np.zeros((3, 1536), np.float32), qkv], axis=0)
    maps = []
    for i in range(NCORE):
        seg = pad[i * TOK:(i + 1) * TOK + 3]
        m = {
            "qkv_fm": np.ascontiguousarray(seg.T.reshape(12, 128, TOK + 3)),
            "convw": np.ascontiguousarray(conv_w[0].T.reshape(12, 128, 4).transpose(1, 0, 2)),
            "ba": np.ascontiguousarray(proj[i * TOK:(i + 1) * TOK, 2048:2056].reshape(NT, 128, 8).transpose(1, 0, 2)),
            "alog": np.ascontiguousarray(np.broadcast_to(a_log[0][None, :], (128, 4))),
            "dtb": np.ascontiguousarray(np.broadcast_to(dt_bias[0][None, :], (128, 4))),
        }
        maps.append(m)
    return maps


def run_dn(proj, conv_w, a_log, dt_bias):
    maps = dn_inputs(proj, conv_w, a_log, dt_bias)
    r1 = run_bass_kernel_spmd(build_dn(1), maps, core_ids=list(range(NCORE)))
    st = np.stack([r["st_out"] for r in r1.results], axis=0)
    bb_all = np.ascontiguousarray(st[:, :, :, 0:128])
    tt_all = np.ascontiguousarray(st[:, :, :, 128:256].transpose(0, 3, 2, 1))
    maps2 = []
    for i in range(NCORE):
        fl = np.zeros((128, 8), np.float32)
        fl[:, :i] = 1.0
        maps2.append({"z": np.ascontiguousarray(proj[i * TOK:(i + 1) * TOK, 1536:2048]), "tt_all": tt_all, "bb_all": bb_all, "flags": fl,
                      "prod": r1.results[i]["prod"], "scal": r1.results[i]["scal"]})
    r2 = run_bass_kernel_spmd(build_dn(2), maps2, core_ids=list(range(NCORE)))
    return np.concatenate([r["o_dn"] for r in r2.results], axis=0), st


def build_rope():
    nc = bass.Bass("TRN2", target_bir_lowering=False)
    q_in = nc.dram_tensor("q", [TOK, 512], F32, kind="ExternalInput").ap()
    k_in = nc.dram_tensor("k", [TOK, 512], F32, kind="ExternalInput").ap()
    pos_in = nc.dram_tensor("pos", [128, NT], mybir.dt.int32, kind="ExternalInput").ap()
    invf_in = nc.dram_tensor("invf", [128, 32], F32, kind="ExternalInput").ap()
    v_in = nc.dram_tensor("v", [TOK, 512], F32, kind="ExternalInput").ap()
    q_out = nc.dram_tensor("q_r", [TOK, 512], BF16, kind="ExternalOutput").ap()
    k_out = nc.dram_tensor("k_r", [TOK, 512], BF16, kind="ExternalOutput").ap()
    v_out = nc.dram_tensor("v_b", [TOK, 512], BF16, kind="ExternalOutput").ap()
    S = Sched(nc)
    A = nc.alloc_sbuf_tensor
    V, G, AC = nc.vector, nc.gpsimd, nc.scalar
    posi = A("posi", [128, NT], mybir.dt.int32); posf = A("posf", [128, NT], F32); invf = A("invf_sb", [128, 32], F32)
    A3 = [128, NT, 32]
    ang = A("ang", A3, F32); sn = A("sn", A3, F32); cs = A("cs", A3, F32); tmp = A("tmpa", A3, F32)
    S.dma("sp", posi[:], pos_in[:, :], writes=["posi"], key="c")
    S.dma("sp", invf[:], invf_in[:, :], writes=["invf"], key="c")
    S.op("dve", lambda: V.tensor_copy(out=posf[:], in_=posi[:]), reads=["posi"], writes=["posf"])
    S.op("dve", lambda: V.tensor_tensor(out=ang[:], in0=posf[:, :].unsqueeze(2).to_broadcast(A3),
                                        in1=invf[:, :].unsqueeze(1).to_broadcast(A3), op=ALU.mult), reads=["posf", "invf"], writes=["ang"])
    ki = A("ki", A3, mybir.dt.int32); kf = A("kf", A3, F32); xs = A("xs", A3, F32); stp = A("stp", A3, F32)
    for dst, off, nm in ((sn, 0.0, "sn"), (cs, 0.5 * PI, "cs")):
        S.op("dve", lambda off=off: V.tensor_scalar(out=xs[:], in0=ang[:], scalar1=off, scalar2=None, op0=ALU.add), reads=["ang"], writes=["xs"])
        S.op("dve", lambda: V.tensor_scalar(out=tmp[:], in0=xs[:], scalar1=1.0 / (2 * PI), scalar2=None, op0=ALU.mult), reads=["xs"], writes=["tmpa"])
        S.op("dve", lambda: V.tensor_copy(out=ki[:], in_=tmp[:]), reads=["tmpa"], writes=["ki"])
        S.op("dve", lambda: V.tensor_copy(out=kf[:], in_=ki[:]), reads=["ki"], writes=["kf"])
        S.op("dve", lambda: V.scalar_tensor_tensor(out=tmp[:], in0=kf[:], scalar=-2 * PI, in1=xs[:], op0=ALU.mult, op1=ALU.add),
             reads=["kf", "xs"], writes=["tmpa"])
        S.op("dve", lambda: V.tensor_scalar(out=stp[:], in0=tmp[:], scalar1=-PI, scalar2=1e30, op0=ALU.add, op1=ALU.mult), reads=["tmpa"], writes=["stp"])
        S.op("dve", lambda: V.tensor_scalar(out=stp[:], in0=stp[:], scalar1=0.0, scalar2=1.0, op0=ALU.max, op1=ALU.min), reads=["stp"], writes=["stp"])
        S.op("dve", lambda: V.scalar_tensor_tensor(out=tmp[:], in0=stp[:], scalar=-2 * PI, in1=tmp[:], op0=ALU.mult, op1=ALU.add),
             reads=["stp", "tmpa"], writes=["tmpa"])
        S.op("dve", lambda: V.tensor_scalar(out=tmp[:], in0=tmp[:], scalar1=-PI, scalar2=PI, op0=ALU.max, op1=ALU.min), reads=["tmpa"], writes=["tmpa"])
        S.op("act", lambda dst=dst: AC.activation(out=dst[:], in_=tmp[:], func=AF.Sin), reads=["tmpa"], writes=[nm])
    xt = [A("xr%d" % i, [128, 512], F32) for i in range(2)]
    t1 = A("t1", [128, 8, 2, 32], F32); t2 = A("t2", [128, 8, 2, 32], F32); ot = [A("or%d" % i, [128, 512], BF16) for i in range(2)]
    B4 = [128, 8, 2, 32]; B3 = [128, 8, 32]
    it = 0
    for src, dstd in ((q_in, q_out), (k_in, k_out)):
        for t in range(NT):
            b = it % 2; it += 1
            ts = slice(t * 128, (t + 1) * 128)
            S.dma("sp" if b == 0 else "act", xt[b][:], src[ts, :], writes=["xr%d" % b], key="x%d" % b)
            xv = xt[b][:, :].rearrange("p (h two i) -> p h two i", h=8, two=2)
            cb = cs[:, t, :].unsqueeze(1).unsqueeze(1).to_broadcast(B4)
            S.op("dve", lambda xv=xv, cb=cb: V.tensor_tensor(out=t1[:], in0=xv, in1=cb, op=ALU.mult), reads=["xr%d" % b, "cs"], writes=["t1"])
            sb3 = sn[:, t, :].unsqueeze(1).to_broadcast(B3)
            S.op("pool", lambda xv=xv, sb3=sb3: G.tensor_tensor(out=t2[:, :, 0, :], in0=xv[:, :, 1, :], in1=sb3, op=ALU.mult),
                 reads=["xr%d" % b, "sn"], writes=["t2a"])
            S.op("pool", lambda xv=xv, sb3=sb3: G.tensor_tensor(out=t2[:, :, 1, :], in0=xv[:, :, 0, :], in1=sb3, op=ALU.mult),
                 reads=["xr%d" % b, "sn"], writes=["t2b"])
            ov = ot[b][:, :].rearrange("p (h two i) -> p h two i", h=8, two=2)
            S.op("dve", lambda ov=ov: V.tensor_tensor(out=ov[:, :, 0, :], in0=t1[:, :, 0, :], in1=t2[:, :, 0, :], op=ALU.subtract),
                 reads=["t1", "t2a"], writes=["or%da" % b])
            S.op("dve", lambda ov=ov: V.tensor_tensor(out=ov[:, :, 1, :], in0=t1[:, :, 1, :], in1=t2[:, :, 1, :], op=ALU.add),
                 reads=["t1", "t2b"], writes=["or%db" % b])
            S.dma("sp", dstd[ts, :], ot[b][:], reads=["or%da" % b, "or%db" % b], key="st%d" % b)
    vt = [A("vt%d" % i, [128, 512], F32) for i in range(2)]; vo = [A("vo%d" % i, [128, 512], BF16) for i in range(2)]
    for t in range(NT):
        b = t % 2
        ts = slice(t * 128, (t + 1) * 128)
        S.dma("act", vt[b][:], v_in[ts, :], writes=["vt%d" % b], key="v%d" % b)
        S.op("act", lambda b=b: AC.copy(out=vo[b][:], in_=vt[b][:]), reads=["vt%d" % b], writes=["vo%d" % b])
        S.dma("sp", v_out[ts, :], vo[b][:], reads=["vo%d" % b], key="sv%d" % b)
    S.wait_all("sp")
    return nc


NBLK = 16
NEG = -30000.0


def build_attn():
    nc = bass.Bass("TRN2", target_bir_lowering=False)
    NB = 3 * NBLK
    qT_in = nc.dram_tensor("qT", [NB, 4, 128, 128], BF16, kind="ExternalInput").ap()
    kT_in = nc.dram_tensor("kT", [NB, 4, 128, 256], BF16, kind="ExternalInput").ap()
    v_in = nc.dram_tensor("v", [NB, 2, 128, 512], BF16, kind="ExternalInput").ap()
    fl_in = nc.dram_tensor("bflag", [128, NB], F32, kind="ExternalInput").ap()
    o_out = nc.dram_tensor("o_un", [NB, 128, 512], F32, kind="ExternalOutput").ap()
    l_out = nc.dram_tensor("l_un", [NB, 128, 8], F32, kind="ExternalOutput").ap()
    m_out = nc.dram_tensor("m_un", [NB, 128, 8], F32, kind="ExternalOutput").ap()
    S = Sched(nc)
    A = nc.alloc_sbuf_tensor
    V, G, AC, PE = nc.vector, nc.gpsimd, nc.scalar, nc.tensor
    ident, identf = make_ident(nc, S)
    mb = A("mb", [128, 256], F32)
    S.op("pool", lambda: G.memset(mb[:], 0.0), writes=["mb"])
    S.op("pool", lambda: G.affine_select(out=mb[:], in_=mb[:], pattern=[[1, 256]], compare_op=ALU.is_ge, fill=NEG, base=0,
                                         channel_multiplier=-1), reads=["mb"], writes=["mb"])
    S.op("pool", lambda: G.affine_select(out=mb[:], in_=mb[:], pattern=[[-1, 256]], compare_op=ALU.is_ge, fill=NEG, base=128,
                                         channel_multiplier=1), reads=["mb"], writes=["mb"])
    fl = A("fl", [128, NB], F32); fb = A("fb", [128, NB], F32)
    S.dma("sp", fl[:], fl_in[:, :], writes=["fl"], key="c")
    S.op("dve", lambda: V.tensor_scalar(out=fb[:], in0=fl[:], scalar1=-1.0, scalar2=-8.0 * NEG, op0=ALU.add, op1=ALU.mult), reads=["fl"], writes=["fb"])
    mb8 = A("mb8", [128, 256], F32)
    S.op("dve", lambda: V.tensor_scalar(out=mb8[:], in0=mb[:], scalar1=8.0, scalar2=None, op0=ALU.mult), reads=["mb"], writes=["mb8"])
    qb = [A("qb%d" % i, [128, 4, 128], BF16) for i in range(2)]; kb = [A("kb%d" % i, [128, 4, 256], BF16) for i in range(2)]
    vb = [A("vbb%d" % i, [128, 2, 512], BF16) for i in range(2)]; mbe = [A("mbe%d" % i, [128, 256], BF16) for i in range(2)]
    mx = [A("mx%d" % i, [128, 8], F32) for i in range(2)]; nmx = [A("nmx%d" % i, [128, 8], F32) for i in range(2)]
    ls = [A("ls%d" % i, [128, 8], F32) for i in range(2)]; ob = [A("ob%d" % i, [128, 512], F32) for i in range(2)]
    NR = 5
    prL = [A("pr%d" % i, [128, 256], BF16) for i in range(NR)]; mx8 = [A("mx8_%d" % i, [128, 8], F32) for i in range(2)]
    prTL = [A("prT%d" % i, [128, 2, 128], BF16) for i in range(NR)]
    NPS = 3
    ps = [nc.alloc_psum_tensor("ps%d" % i, [128, 256], F32) for i in range(NPS)]
    pt = [nc.alloc_psum_tensor("pt%d" % i, [128, 256], BF16) for i in range(2)]
    po = [nc.alloc_psum_tensor("po%d" % i, [128, 64], F32) for i in range(2)]

    def load_block(blk):
        b = blk % 2
        S.dma("sp", qb[b][:], qT_in[blk].rearrange("g p q -> p g q"), writes=["qb%d" % b], key="q%d" % b)
        S.dma("sp", kb[b][:], kT_in[blk].rearrange("g p k -> p g k"), writes=["kb%d" % b], key="k%d" % b)
        S.dma("sp", vb[b][:], v_in[blk].rearrange("c p f -> p c f"), writes=["vbb%d" % b], key="v%d" % b)
        S.op("pool", lambda: G.tensor_copy(out=mbe[b][:, 128:256], in_=mb8[:, 128:256]), reads=["mb8"], writes=["mbe_hi%d" % b])
        S.op("pool", lambda: G.tensor_scalar(out=mbe[b][:, 0:128], in0=mb8[:, 0:128], scalar1=fb[:, blk:blk + 1], scalar2=None, op0=ALU.add),
             reads=["mb8", "fb"], writes=["mbe_lo%d" % b])

    def names(i):
        blk, h = divmod(i, 8)
        b, ri, hb = blk % 2, i % NR, i % 2
        return blk, h, b, ri, hb

    def stage_a(i):
        blk, h, b, ri, hb = names(i)
        pi = i % NPS
        g2, o2 = h // 2, (h % 2) * 64
        S.op("pe", lambda: PE.matmul(ps[pi][:, :], lhsT=qb[b][o2:o2 + 64, g2, :], rhs=kb[b][o2:o2 + 64, g2, :], start=True, stop=False),
             reads=["qb%d" % b, "kb%d" % b], writes=["ps%d" % pi])
        S.op("pe", lambda: PE.matmul(ps[pi][:, :], lhsT=ident[:, :], rhs=mbe[b][:, :], start=False, stop=True),
             reads=["ident", "mbe_lo%d" % b, "mbe_hi%d" % b], writes=["ps%d" % pi])
        S.op("dve", lambda: V.tensor_reduce(out=mx8[b][:, h:h + 1], in_=ps[pi][:, :], axis=AX, op=ALU.max), reads=["ps%d" % pi], writes=["mx8_%d_%d" % (b, h)])
        S.op("dve", lambda: V.tensor_scalar(out=nmx[b][:, h:h + 1], in0=mx8[b][:, h:h + 1], scalar1=-0.125, scalar2=None, op0=ALU.mult),
             reads=["mx8_%d_%d" % (b, h)], writes=["nmx%d_%d" % (b, h)])

    def stage_b(i):
        blk, h, b, ri, hb = names(i)
        pi = i % NPS
        S.op("act", lambda: AC.activation(out=prL[ri][:], in_=ps[pi][:, :], func=AF.Exp, scale=0.125, bias=nmx[b][:, h:h + 1], accum_out=ls[b][:, h:h + 1]),
             reads=["ps%d" % pi, "nmx%d_%d" % (b, h)], writes=["pr%d" % ri, "ls%d_%d" % (b, h)])

    def stage_c(i):
        blk, h, b, ri, hb = names(i)
        for c2 in range(2):
            S.op("pe", lambda c2=c2: PE.transpose(out=pt[hb][:, c2 * 128:(c2 + 1) * 128], in_=prL[ri][:, c2 * 128:(c2 + 1) * 128], identity=ident[:]),
                 reads=["pr%d" % ri, "ident"], writes=["pt%d" % hb])
        S.op("act", lambda: AC.copy(out=prTL[ri][:], in_=pt[hb][:, :].rearrange("p (c q) -> p c q", c=2)), reads=["pt%d" % hb], writes=["prT%d" % ri])

    def stage_d(i):
        blk, h, b, ri, hb = names(i)
        for c2 in range(2):
            S.op("pe", lambda c2=c2: PE.matmul(po[hb][:, :], lhsT=prTL[ri][:, c2, :], rhs=vb[b][:, c2, h * 64:(h + 1) * 64], start=(c2 == 0), stop=(c2 == 1)),
                 reads=["prT%d" % ri, "vbb%d" % b], writes=["po%d" % hb])
        S.op("dve", lambda: V.tensor_copy(out=ob[b][:, h * 64:(h + 1) * 64], in_=po[hb][:, :]), reads=["po%d" % hb], writes=["ob%d_%d" % (b, h)])
        if h == 7:
            S.op("dve", lambda: V.tensor_scalar(out=mx[b][:], in0=mx8[b][:], scalar1=0.125, scalar2=None, op0=ALU.mult),
                 reads=["mx8_%d_%d" % (b, k) for k in range(8)], writes=["mx%d_%d" % (b, k) for k in range(8)])
            S.dma("pool", o_out[blk], ob[b][:], reads=["ob%d_%d" % (b, k) for k in range(8)], key="sto%d" % b)
            S.dma("pool", l_out[blk], ls[b][:], reads=["ls%d_%d" % (b, k) for k in range(8)], key="stl%d" % b)
            S.dma("pool", m_out[blk], mx[b][:], reads=["mx%d_%d" % (b, k) for k in range(8)], key="stm%d" % b)

    NI = NB * 8
    load_block(0)
    for s_ in range(NI + 3):
        if s_ % 8 == 3 and s_ // 8 + 1 < NB:
            load_block(s_ // 8 + 1)
        if s_ < NI:
            stage_a(s_)
        if 0 <= s_ - 1 < NI:
            stage_b(s_ - 1)
        if 0 <= s_ - 2 < NI:
            stage_c(s_ - 2)
        if 0 <= s_ - 3 < NI:
            stage_d(s_ - 3)
    S.wait_all("sp")
    return nc


PATTERN_DIL = (1, 4, 16)


def attn_layout(q_r, k_r, v):
    per_core = [dict() for _ in range(NCORE)]
    qTs, kTs, vs, fls, perm = [], [], [], [], []
    for d in PATTERN_DIL:
        L = S_TOT // d
        nb = L // 128
        tok = (np.arange(L)[None, :] * d + np.arange(d)[:, None])
        qs = q_r[tok].reshape(d, nb, 128, 8, 64)
        ks = k_r[tok].reshape(d, nb, 128, 8, 64)
        vv = v[tok].reshape(d, nb, 128, 512)
        kprev = np.concatenate([np.zeros_like(ks[:, :1]), ks[:, :-1]], axis=1)
        vprev = np.concatenate([np.zeros_like(vv[:, :1]), vv[:, :-1]], axis=1)
        k2 = np.concatenate([kprev, ks], axis=2).reshape(d * nb, 256, 8, 64)
        v2 = np.stack([vprev, vv], axis=2).reshape(d * nb, 2, 128, 512)
        flag = np.ones((d, nb), np.float32); flag[:, 0] = 0.0
        qT = qs.reshape(d * nb, 128, 4, 128).transpose(0, 2, 3, 1)
        kT = k2.reshape(d * nb, 256, 4, 128).transpose(0, 2, 3, 1)
        qTs.append(qT); kTs.append(kT); vs.append(v2); fls.append(flag.reshape(-1)); perm.append(tok.reshape(-1))
    maps = []
    for i in range(NCORE):
        sl = slice(i * NBLK, (i + 1) * NBLK)
        m = {
            "qT": np.ascontiguousarray(np.concatenate([a[sl] for a in qTs], axis=0)),
            "kT": np.ascontiguousarray(np.concatenate([a[sl] for a in kTs], axis=0)),
            "v": np.ascontiguousarray(np.concatenate([a[sl] for a in vs], axis=0)),
            "bflag": np.ascontiguousarray(np.broadcast_to(np.concatenate([f[sl] for f in fls])[None, :], (128, 3 * NBLK))),
        }
        maps.append(m)
    return maps, perm


def run_attn(q_r, k_r, v):
    maps, perm = attn_layout(q_r, k_r, v)
    res = run_bass_kernel_spmd(build_attn(), maps, core_ids=list(range(NCORE)))
    outs = []
    for p in range(3):
        o = np.concatenate([r["o_un"][p * NBLK:(p + 1) * NBLK] for r in res.results], axis=0).reshape(S_TOT, 512)
        l = np.concatenate([r["l_un"][p * NBLK:(p + 1) * NBLK] for r in res.results], axis=0).reshape(S_TOT, 8)
        m = np.concatenate([r["m_un"][p * NBLK:(p + 1) * NBLK] for r in res.results], axis=0).reshape(S_TOT, 8)
        inv = np.empty(S_TOT, np.int64); inv[perm[p]] = np.arange(S_TOT)
        outs.append((np.ascontiguousarray(o[inv]), np.ascontiguousarray(l[inv]), np.ascontiguousarray(m[inv])))
    return outs


def build_final():
    nc = bass.Bass("TRN2", target_bir_lowering=False)
    x_in = nc.dram_tensor("x", [TOK, D], F32, kind="ExternalInput").ap()
    odn_in = nc.dram_tensor("o_dn", [TOK, 512], F32, kind="ExternalInput").ap()
    az_in = nc.dram_tensor("at_z", [TOK, 512], F32, kind="ExternalInput").ap()
    o_in = [nc.dram_tensor("ao%d" % p, [TOK, 512], F32, kind="ExternalInput").ap() for p in range(3)]
    l_in = [nc.dram_tensor("al%d" % p, [TOK, 8], F32, kind="ExternalInput").ap() for p in range(3)]
    m_in = [nc.dram_tensor("am%d" % p, [TOK, 8], F32, kind="ExternalInput").ap() for p in range(3)]
    wout_in = nc.dram_tensor("w_out", [D, D], F32, kind="ExternalInput").ap()
    nwc_in = nc.dram_tensor("mixnw", [128, 8], F32, kind="ExternalInput").ap()
    gate_in = nc.dram_tensor("gate_b", [128, D], F32, kind="ExternalInput").ap()
    fnw_in = nc.dram_tensor("fnw_b", [128, D], F32, kind="ExternalInput").ap()
    y_out = nc.dram_tensor("y", [TOK, D], F32, kind="ExternalOutput").ap()
    S = Sched(nc)
    A = nc.alloc_sbuf_tensor
    V, G, AC, PE = nc.vector, nc.gpsimd, nc.scalar, nc.tensor
    ident, identf = make_ident(nc, S)
    nwc = A("nwc", [128, 8], F32); gate = A("gate", [128, D], F32); fnw = A("fnw", [128, D], F32)
    S.dma("sp", nwc[:], nwc_in[:, :], writes=["nwc"], key="c")
    S.dma("sp", gate[:], gate_in[:, :], writes=["gate"], key="c")
    S.dma("sp", fnw[:], fnw_in[:, :], writes=["fnw"], key="c")
    wst = A("wst", [128, 8, D], F32); wo = A("wo", [128, 8, D], BF16)
    S.dma("act", wst[:], wout_in.rearrange("(kt p) m -> p kt m", p=128), writes=["wst"], key="w")
    for kt in range(8):
        if kt % 2 == 0:
            S.op("dve", lambda kt=kt: V.tensor_scalar(out=wo[:, kt, :], in0=wst[:, kt, :], scalar1=nwc[:, kt:kt + 1], scalar2=None, op0=ALU.mult),
                 reads=["wst", "nwc"], writes=["wo%d" % kt])
        else:
            S.op("act", lambda kt=kt: AC.activation(out=wo[:, kt, :], in_=wst[:, kt, :], func=AF.Copy, scale=nwc[:, kt:kt + 1]),
                 reads=["wst", "nwc"], writes=["wo%d" % kt])
    H8 = [128, 8, 64]
    xtL = [A("xt%d" % i, [128, D], F32) for i in range(2)]; odL = [A("od%d" % i, [128, 512], F32) for i in range(2)]
    azL = [A("az%d" % i, [128, 512], F32) for i in range(2)]
    aoL = [[A("aot%d_%d" % (p, i), H8, F32) for p in range(3)] for i in range(2)]
    alL = [[A("alt%d_%d" % (p, i), [128, 8], F32) for p in range(3)] for i in range(2)]
    amL = [[A("amt%d_%d" % (p, i), [128, 8], F32) for p in range(3)] for i in range(2)]

    def load_tile(t):
        i = t % 2
        ts = slice(t * 128, (t + 1) * 128)
        S.dma("sp", xtL[i][:], x_in[ts, :], writes=["xt%d" % i], key="x%d" % i)
        S.dma("act", odL[i][:], odn_in[ts, :], writes=["od%d" % i], key="od%d" % i)
        S.dma("act", azL[i][:], az_in[ts, :], writes=["az%d" % i], key="az%d" % i)
        for p in range(3):
            S.dma("sp", aoL[i][p][:], o_in[p][ts, :].rearrange("p (h e) -> p h e", h=8), writes=["ao%d_%d" % (p, i)], key="ao%d_%d" % (p, i))
            S.dma("pool", alL[i][p][:], l_in[p][ts, :], writes=["al%d_%d" % (p, i)], key="al%d_%d" % (p, i))
            S.dma("pool", amL[i][p][:], m_in[p][ts, :], writes=["am%d_%d" % (p, i)], key="am%d_%d" % (p, i))
    mm_ = A("mm_", [128, 8], F32); wp = [A("wp%d" % p, [128, 8], F32) for p in range(3)]; lt = A("lt", [128, 8], F32)
    oa = A("oa", H8, F32); tmp8 = A("tmp8", H8, F32); sqa = A("sqa", H8, F32); ssa = A("ssa", [128, 8], F32); ra = A("ra", [128, 8], F32)
    mixbL = [A("mixb%d" % i, [128, D], BF16) for i in range(2)]; mixT = A("mixT", [128, 8, 128], BF16); yo = A("yo", [128, D], F32); junk = A("junkf", [128, D], F32)
    ssf = A("ssf", [128, 1], F32)
    pT = nc.alloc_psum_tensor("pT", [128, 8, 128], BF16)
    pm = [nc.alloc_psum_tensor("pm%d" % i, [128, 512], F32) for i in range(2)]

    def bc8(a):
        return a.unsqueeze(2).to_broadcast(H8)
    def stage_a(t):
        i_ = t % 2
        od, az, ao, al, am, mixb = odL[i_], azL[i_], aoL[i_], alL[i_], amL[i_], mixbL[i_]
        kod, kaz = "od%d" % i_, "az%d" % i_
        kao = ["ao%d_%d" % (p, i_) for p in range(3)]; kal = ["al%d_%d" % (p, i_) for p in range(3)]; kam = ["am%d_%d" % (p, i_) for p in range(3)]
        kmlo, kmhi = "mixb_lo%d" % i_, "mixb_hi%d" % i_
        S.op("dve", lambda: V.tensor_tensor(out=mm_[:], in0=am[0][:], in1=am[1][:], op=ALU.max), reads=[kam[0], kam[1]], writes=["mm_"])
        S.op("dve", lambda: V.tensor_tensor(out=mm_[:], in0=mm_[:], in1=am[2][:], op=ALU.max), reads=["mm_", kam[2]], writes=["mm_"])
        for p in range(3):
            S.op("dve", lambda p=p: V.tensor_tensor(out=wp[p][:], in0=am[p][:], in1=mm_[:], op=ALU.subtract), reads=[kam[p], "mm_"], writes=["wp%d" % p])
            S.op("act", lambda p=p: AC.activation(out=wp[p][:], in_=wp[p][:], func=AF.Exp), reads=["wp%d" % p], writes=["wp%d" % p])
        S.op("dve", lambda: V.tensor_tensor(out=lt[:], in0=al[0][:], in1=wp[0][:], op=ALU.mult), reads=[kal[0], "wp0"], writes=["lt"])
        S.op("dve", lambda: V.tensor_tensor(out=oa[:], in0=ao[0][:], in1=bc8(wp[0][:, :]), op=ALU.mult), reads=[kao[0], "wp0"], writes=["oa"])
        for p in (1, 2):
            S.op("dve", lambda p=p: V.tensor_tensor(out=ra[:], in0=al[p][:], in1=wp[p][:], op=ALU.mult), reads=[kal[p], "wp%d" % p], writes=["ra"])
            S.op("dve", lambda: V.tensor_tensor(out=lt[:], in0=lt[:], in1=ra[:], op=ALU.add), reads=["lt", "ra"], writes=["lt"])
            S.op("pool", lambda p=p: G.tensor_tensor(out=tmp8[:], in0=ao[p][:], in1=bc8(wp[p][:, :]), op=ALU.mult), reads=[kao[p], "wp%d" % p], writes=["tmp8"])
            S.op("dve", lambda: V.tensor_tensor(out=oa[:], in0=oa[:], in1=tmp8[:], op=ALU.add), reads=["oa", "tmp8"], writes=["oa"])
        S.op("dve", lambda: V.reciprocal(out=lt[:], in_=lt[:]), reads=["lt"], writes=["lt"])
        S.op("dve", lambda: V.tensor_tensor(out=oa[:], in0=oa[:], in1=bc8(lt[:, :]), op=ALU.mult), reads=["oa", "lt"], writes=["oa"])
        S.op("act", lambda: AC.activation(out=sqa[:], in_=oa[:], func=AF.Square), reads=["oa"], writes=["sqa"])
        S.op("dve", lambda: V.tensor_reduce(out=ssa[:], in_=sqa[:], axis=AX, op=ALU.add), reads=["sqa"], writes=["ssa"])
        S.op("dve", lambda: V.tensor_scalar(out=ssa[:], in0=ssa[:], scalar1=1.0 / 64, scalar2=EPS, op0=ALU.mult, op1=ALU.add), reads=["ssa"], writes=["ssa"])
        S.op("act", lambda: AC.activation(out=ssa[:], in_=ssa[:], func=AF.Sqrt), reads=["ssa"], writes=["ssa"])
        S.op("dve", lambda: V.reciprocal(out=ra[:], in_=ssa[:]), reads=["ssa"], writes=["ra"])
        S.op("dve", lambda: V.tensor_tensor(out=oa[:], in0=oa[:], in1=bc8(ra[:, :]), op=ALU.mult), reads=["oa", "ra"], writes=["oa"])
        S.op("act", lambda: AC.activation(out=az[:], in_=az[:], func=AF.Silu), reads=[kaz], writes=[kaz])
        S.op("dve", lambda: V.tensor_tensor(out=mixb[:, 512:1024], in0=oa[:].rearrange("p h e -> p (h e)"), in1=az[:], op=ALU.mult),
             reads=["oa", kaz], writes=[kmhi])
        S.op("act", lambda: AC.copy(out=mixb[:, 0:512], in_=od[:]), reads=[kod], writes=[kmlo])

    def stage_b(t):
        i_ = t % 2
        ts = slice(t * 128, (t + 1) * 128)
        xt, mixb, kx = xtL[i_], mixbL[i_], "xt%d" % i_
        kmlo, kmhi = "mixb_lo%d" % i_, "mixb_hi%d" % i_
        for kt in range(8):
            S.op("pe", lambda kt=kt: PE.transpose(out=pT[:, kt, :], in_=mixb[:, kt * 128:(kt + 1) * 128], identity=ident[:]),
                 reads=[kmlo, kmhi, "ident"], writes=["pT"])
        S.op("act", lambda: AC.copy(out=mixT[:], in_=pT[:]), reads=["pT"], writes=["mixT"])
        for cg in range(2):
            for kt in range(8):
                S.op("pe", lambda cg=cg, kt=kt: PE.matmul(pm[cg][:, :], lhsT=mixT[:, kt, :], rhs=wo[:, kt, cg * 512:(cg + 1) * 512], start=(kt == 0), stop=(kt == 7)),
                     reads=["mixT"] + ["wo%d" % k for k in range(8)], writes=["pm%d" % cg])
            cs_ = slice(cg * 512, (cg + 1) * 512)
            S.op("dve", lambda cg=cg, cs_=cs_: V.tensor_tensor(out=yo[:, cs_], in0=pm[cg][:, :], in1=gate[:, cs_], op=ALU.mult), reads=["pm%d" % cg, "gate"], writes=["yo%d" % cg])
            S.op("dve", lambda cs_=cs_: V.tensor_tensor(out=yo[:, cs_], in0=yo[:, cs_], in1=xt[:, cs_], op=ALU.add), reads=["yo%d" % cg, kx], writes=["yo%d" % cg])
        S.op("act", lambda: AC.activation(out=junk[:], in_=yo[:], func=AF.Square, accum_out=ssf[:]), reads=["yo0", "yo1"], writes=["junkf", "ssf"])
        S.op("dve", lambda: V.tensor_scalar(out=ssf[:], in0=ssf[:], scalar1=1.0 / D, scalar2=EPS, op0=ALU.mult, op1=ALU.add), reads=["ssf"], writes=["ssf"])
        S.op("act", lambda: AC.activation(out=ssf[:], in_=ssf[:], func=AF.Sqrt), reads=["ssf"], writes=["ssf"])
        S.op("dve", lambda: V.reciprocal(out=ssf[:], in_=ssf[:]), reads=["ssf"], writes=["ssf"])
        S.op("dve", lambda: V.scalar_tensor_tensor(out=yo[:], in0=yo[:], scalar=ssf[:, 0:1], in1=fnw[:], op0=ALU.mult, op1=ALU.mult),
             reads=["yo0", "yo1", "ssf", "fnw"], writes=["yo0", "yo1"])
        S.dma("sp", y_out[ts, :], yo[:], reads=["yo0", "yo1"], key="st")

    load_tile(0)
    if NT > 1:
        load_tile(1)
    stage_a(0)
    for t in range(NT):
        if t + 1 < NT:
            stage_a(t + 1)
        stage_b(t)
        if t + 2 < NT:
            load_tile(t + 2)
    S.wait_all("sp")
    return nc


def run_all(x, c, positions, w_mod, b_mod, norm_w, w_in, conv_w, a_log, dt_bias, dn_norm_w, at_norm_w, w_out, final_norm_w):
    cores = list(range(NCORE))
    proj, modT, (q_r, k_r, v_b) = run_l1(x, c, w_mod, b_mod, norm_w, w_in, positions)
    o_dn, _ = run_dn(proj, conv_w, a_log, dt_bias)
    pats = run_attn(q_r, k_r, v_b)
    gate_row = modT.T.reshape(-1)[2048:3072]
    mixnw = np.concatenate([np.tile(dn_norm_w[0], 4), np.tile(at_norm_w[0], 8)]).astype(np.float32)
    maps = []
    for i in cores:
        sl = slice(i * TOK, (i + 1) * TOK)
        m = {"x": np.ascontiguousarray(x[0, sl]), "o_dn": np.ascontiguousarray(o_dn[sl]), "at_z": np.ascontiguousarray(proj[sl, 3592:4104]),
             "w_out": np.ascontiguousarray(w_out[0]), "mixnw": np.ascontiguousarray(mixnw.reshape(8, 128).T),
             "gate_b": np.ascontiguousarray(np.broadcast_to(gate_row[None, :], (128, D))),
             "fnw_b": np.ascontiguousarray(np.broadcast_to(final_norm_w[None, :], (128, D)))}
        for p in range(3):
            m["ao%d" % p], m["al%d" % p], m["am%d" % p] = (np.ascontiguousarray(a[sl]) for a in pats[p])
        maps.append(m)
    rf = run_bass_kernel_spmd(build_final(), maps, core_ids=cores)
    y = np.concatenate([r["y"] for r in rf.results], axis=0)
    return y.reshape(1, S_TOT, D).astype(np.float32)


def kernel(**inputs):
    return run_all(**{k: np.asarray(v) for k, v in inputs.items()})
```

```python
import numpy as np
import concourse.bass as bass
import concourse.mybir as mybir
from concourse.bass_utils import run_bass_kernel_spmd

F32 = mybir.dt.float32
BF16 = mybir.dt.bfloat16
AF = mybir.ActivationFunctionType
ALU = mybir.AluOpType

NCORE = 8
S_TOT = 16384
D = 1024
TOK = S_TOT // NCORE
NT = TOK // 128
IN_COLS = 4104
EPS = 1e-6
PI = float(np.pi)


class Sched:
    def __init__(self, nc):
        self.nc = nc
        self.eng = {"pe": nc.tensor, "act": nc.scalar, "dve": nc.vector, "pool": nc.gpsimd, "sp": nc.sync}
        self.sem = {k: nc.alloc_semaphore("prog_" + k) for k in self.eng}
        self.cnt = {k: 0 for k in self.eng}
        self.waited = {k: {} for k in self.eng}
        self.dsem, self.dcnt, self.bufs = {}, {}, {}
        self.semh = {("e", k): self.sem[k] for k in self.eng}
        self.ns = ""
        self.shared = set()

    def _k(self, keys):
        if not self.ns:
            return list(keys)
        return [k if k in self.shared else k + self.ns for k in keys]

    def _deps(self, reads, writes):
        deps = []
        for b in reads:
            st = self.bufs.get(b)
            if st and st["w"]:
                deps.append(st["w"])
        for b in writes:
            st = self.bufs.get(b)
            if st:
                if st["w"]:
                    deps.append(st["w"])
                deps.extend(st["r"])
        return deps

    def _wait(self, e, deps):
        need = {}
        for (sk, v) in deps:
            if e == "pe" and sk == ("e", "pe"):
                continue
            if v > need.get(sk, 0):
                need[sk] = v
        for sk, v in need.items():
            if self.waited[e].get(sk, 0) >= v:
                continue
            self.eng[e].wait_ge(self.semh[sk], v)
            self.waited[e][sk] = v

    def _record(self, tag, reads, writes):
        for b in reads:
            st = self.bufs.setdefault(b, {"w": None, "r": []})
            st["r"].append(tag)
            if len(st["r"]) > 64:
                best = {}
                for (sk, v) in st["r"]:
                    best[sk] = max(best.get(sk, 0), v)
                st["r"] = list(best.items())
        for b in writes:
            self.bufs[b] = {"w": tag, "r": []}

    def op(self, e, fn, reads=(), writes=()):
        reads, writes = self._k(reads), self._k(writes)
        self._wait(e, self._deps(reads, writes))
        ins = fn()
        self.cnt[e] += 1
        ins.then_inc(self.sem[e], 1)
        self._record((("e", e), self.cnt[e]), reads, writes)
        return ins

    def dma(self, q, out, in_, reads=(), writes=(), key="d0"):
        reads, writes = self._k(reads), self._k(writes)
        key = q + "_" + key
        if key not in self.dsem:
            self.dsem[key] = self.nc.alloc_semaphore("dma_" + key)
            self.dcnt[key] = 0
            self.semh[("d", key)] = self.dsem[key]
        deps = self._deps(reads, writes)
        if self.dcnt[key] > 0:
            deps.append((("d", key), self.dcnt[key]))
        self._wait(q, deps)
        ins = self.eng[q].dma_start(out=out, in_=in_)
        self.dcnt[key] += 16
        ins.then_inc(self.dsem[key], 16)
        self._record((("d", key), self.dcnt[key]), reads, writes)
        return ins

    def wait_all(self, e):
        deps = [(("e", k), self.cnt[k]) for k in self.eng if self.cnt[k] > 0 and k != e]
        deps += [(("d", k), v) for k, v in self.dcnt.items()]
        self._wait(e, deps)


def make_ident(nc, S, name="ident"):
    identf = nc.alloc_sbuf_tensor(name + "f", [128, 128], F32)
    ident = nc.alloc_sbuf_tensor(name, [128, 128], BF16)
    S.op("pool", lambda: nc.gpsimd.memset(identf[:], 1.0), writes=[name + "f"])
    S.op("pool", lambda: nc.gpsimd.affine_select(out=identf[:], in_=identf[:], pattern=[[-1, 128]],
                                                  compare_op=ALU.is_equal, fill=0.0, base=0, channel_multiplier=1),
         reads=[name + "f"], writes=[name + "f"])
    S.op("pool", lambda: nc.gpsimd.tensor_copy(out=ident[:], in_=identf[:]), reads=[name + "f"], writes=[name])
    return ident, identf


def build_l1(ntiles=NT):
    nc = bass.Bass("TRN2", target_bir_lowering=False)
    x = nc.dram_tensor("x", [TOK, D], F32, kind="ExternalInput").ap()
    c_in = nc.dram_tensor("c", [128, 8], F32, kind="ExternalInput").ap()
    w_mod = nc.dram_tensor("w_mod", [D, 3 * D], F32, kind="ExternalInput").ap()
    b_mod = nc.dram_tensor("b_mod", [128, 24], F32, kind="ExternalInput").ap()
    norm_w = nc.dram_tensor("norm_w", [128, 8], F32, kind="ExternalInput").ap()
    w_in = nc.dram_tensor("w_in", [D, IN_COLS], F32, kind="ExternalInput").ap()
    proj = nc.dram_tensor("proj", [TOK, IN_COLS], F32, kind="ExternalOutput").ap()
    modT = nc.dram_tensor("modT", [128, 24], F32, kind="ExternalOutput").ap()
    pos_in = nc.dram_tensor("pos", [128, NT], mybir.dt.int32, kind="ExternalInput").ap()
    invf_in = nc.dram_tensor("invf", [128, 32], F32, kind="ExternalInput").ap()
    q_out = nc.dram_tensor("q_r", [TOK, 512], BF16, kind="ExternalOutput").ap()
    k_out = nc.dram_tensor("k_r", [TOK, 512], BF16, kind="ExternalOutput").ap()
    v_out = nc.dram_tensor("v_b", [TOK, 512], BF16, kind="ExternalOutput").ap()
    S = Sched(nc)
    A = nc.alloc_sbuf_tensor
    V, G, AC = nc.vector, nc.gpsimd, nc.scalar
    ident, identf = make_ident(nc, S)

    posi = A("posi", [128, NT], mybir.dt.int32); posf = A("posf", [128, NT], F32); invf = A("invf_sb", [128, 32], F32)
    A3 = [128, NT, 32]
    wst = [A("wst%d" % i, [128, 8, 512], F32) for i in range(2)]
    ang = A("ang", A3, F32); sn = A("sn", A3, F32); cs = A("cs", A3, F32)
    ki = A("ki", A3, mybir.dt.int32)
    tmpa, kf, xs_, stp = (wst[1][:, k_, :].rearrange("p (t i) -> p t i", t=NT) for k_ in range(4))
    S.dma("sp", posi[:], pos_in[:, :], writes=["posi"], key="c")
    S.dma("sp", invf[:], invf_in[:, :], writes=["invf"], key="c")
    S.op("dve", lambda: V.tensor_copy(out=posf[:], in_=posi[:]), reads=["posi"], writes=["posf"])
    S.op("dve", lambda: V.tensor_tensor(out=ang[:], in0=posf[:, :].unsqueeze(2).to_broadcast(A3),
                                        in1=invf[:, :].unsqueeze(1).to_broadcast(A3), op=ALU.mult), reads=["posf", "invf"], writes=["ang"])
    for dst, off, nm in ((sn, 0.0, "sn"), (cs, 0.5 * PI, "cs")):
        S.op("dve", lambda off=off: V.tensor_scalar(out=xs_, in0=ang[:], scalar1=off, scalar2=None, op0=ALU.add), reads=["ang"], writes=["wst1"])
        S.op("dve", lambda: V.tensor_scalar(out=tmpa, in0=xs_, scalar1=1.0 / (2 * PI), scalar2=None, op0=ALU.mult), reads=["wst1"], writes=["wst1"])
        S.op("dve", lambda: V.tensor_copy(out=ki[:], in_=tmpa), reads=["wst1"], writes=["ki"])
        S.op("dve", lambda: V.tensor_copy(out=kf, in_=ki[:]), reads=["ki"], writes=["wst1"])
        S.op("dve", lambda: V.scalar_tensor_tensor(out=tmpa, in0=kf, scalar=-2 * PI, in1=xs_, op0=ALU.mult, op1=ALU.add),
             reads=["wst1", "wst1"], writes=["wst1"])
        S.op("dve", lambda: V.tensor_scalar(out=stp, in0=tmpa, scalar1=-PI, scalar2=1e30, op0=ALU.add, op1=ALU.mult), reads=["wst1"], writes=["wst1"])
        S.op("dve", lambda: V.tensor_scalar(out=stp, in0=stp, scalar1=0.0, scalar2=1.0, op0=ALU.max, op1=ALU.min), reads=["wst1"], writes=["wst1"])
        S.op("dve", lambda: V.scalar_tensor_tensor(out=tmpa, in0=stp, scalar=-2 * PI, in1=tmpa, op0=ALU.mult, op1=ALU.add),
             reads=["wst1", "wst1"], writes=["wst1"])
        S.op("dve", lambda: V.tensor_scalar(out=tmpa, in0=tmpa, scalar1=-PI, scalar2=PI, op0=ALU.max, op1=ALU.min), reads=["wst1"], writes=["wst1"])
        S.op("act", lambda dst=dst: AC.activation(out=dst[:], in_=tmpa, func=AF.Sin), reads=["wst1"], writes=[nm])
    B4 = [128, 8, 2, 32]; B3 = [128, 8, 32]
    rt1 = A("rt1", B4, F32); rt2 = A("rt2", B4, F32)
    qo = [A("qo%d" % i, [128, 512], BF16) for i in range(2)]; ko = [A("ko%d" % i, [128, 512], BF16) for i in range(2)]
    vo = [A("vo%d" % i, [128, 512], BF16) for i in range(2)]

    ct = A("ct", [128, 8], F32)
    sc = A("sc", [128, 8], F32)
    bm = A("bm", [128, 24], F32)
    nw = A("nw", [128, 8], F32)
    md = A("md", [128, 24], F32)
    S.dma("sp", ct[:], c_in[:, :], writes=["ct"], key="c")
    S.dma("sp", bm[:], b_mod[:, :], writes=["bm"], key="c")
    S.dma("sp", nw[:], norm_w[:, :], writes=["nw"], key="c")
    S.op("act", lambda: nc.scalar.activation(out=sc[:], in_=ct[:], func=AF.Silu), reads=["ct"], writes=["sc"])
    wm = [A("wm%d" % i, [128, 8, 256], F32) for i in range(2)]
    pmod = nc.alloc_psum_tensor("pmod", [128, 24], F32)
    wmv = w_mod.rearrange("(kt p) m -> p kt m", p=128)
    for cg in range(12):
        b = cg % 2
        S.dma("sp" if b == 0 else "act", wm[b][:], wmv[:, :, cg * 256:(cg + 1) * 256], writes=["wm%d" % b], key="wm%d" % b)
        for j in range(2):
            mt = cg * 2 + j
            for kt in range(8):
                S.op("pe", lambda b=b, j=j, kt=kt, mt=mt: nc.tensor.matmul(
                    pmod[:, mt:mt + 1], lhsT=wm[b][:, kt, j * 128:(j + 1) * 128], rhs=sc[:, kt:kt + 1],
                    start=(kt == 0), stop=(kt == 7)), reads=["wm%d" % b, "sc"], writes=["pmod"])
    S.op("dve", lambda: nc.vector.tensor_tensor(out=md[:], in0=pmod[:], in1=bm[:], op=ALU.add),
         reads=["pmod", "bm"], writes=["md"])
    S.dma("sp", modT[:, :], md[:], reads=["md"], key="st")
    gcol = A("gcol", [128, 8], F32)
    sgf = A("sgf", [128, 8], F32)
    sgb = A("sgb", [128, 8], BF16)
    S.op("dve", lambda: nc.vector.scalar_tensor_tensor(out=gcol[:], in0=md[:, 8:16], scalar=1.0, in1=nw[:],
                                                       op0=ALU.add, op1=ALU.mult), reads=["md", "nw"], writes=["gcol"])
    S.op("dve", lambda: nc.vector.reciprocal(out=sgf[:], in_=gcol[:]), reads=["gcol"], writes=["sgf"])
    S.op("dve", lambda: nc.vector.tensor_tensor(out=sgb[:], in0=sgf[:], in1=md[:, 0:8], op=ALU.mult),
         reads=["sgf", "md"], writes=["sgb"])

    wb = A("wb", [128, 8, IN_COLS], BF16)
    wiv = w_in.rearrange("(kt p) m -> p kt m", p=128)
    CG = [(i * 512, 512) for i in range(8)] + [(4096, 8)]
    for ci, (c0, cw) in enumerate(CG):
        b = ci % 2
        S.dma("sp" if b == 0 else "act", wst[b][:, :, 0:cw], wiv[:, :, c0:c0 + cw], writes=["wst%d" % b], key="wi%d" % b)
        for kt in range(8):
            if kt % 2 == 0:
                S.op("dve", lambda b=b, kt=kt, c0=c0, cw=cw: nc.vector.tensor_scalar(
                    out=wb[:, kt, c0:c0 + cw], in0=wst[b][:, kt, 0:cw], scalar1=gcol[:, kt:kt + 1], scalar2=None,
                    op0=ALU.mult), reads=["wst%d" % b, "gcol"], writes=["wb_%d" % ci])
            else:
                S.op("act", lambda b=b, kt=kt, c0=c0, cw=cw: nc.scalar.activation(
                    out=wb[:, kt, c0:c0 + cw], in_=wst[b][:, kt, 0:cw], func=AF.Copy, scale=gcol[:, kt:kt + 1]),
                    reads=["wst%d" % b, "gcol"], writes=["wb_%d" % ci])
    ones1 = A("ones1", [1, 128], BF16)
    S.op("pool", lambda: nc.gpsimd.memset(ones1[:], 1.0), writes=["ones1"])
    brow = A("brow", [1, IN_COLS], BF16)
    biasb = A("biasb", [128, IN_COLS], F32)
    pb = nc.alloc_psum_tensor("pb", [128, 512], F32)
    for ci, (c0, cw) in enumerate(CG):
        for kt in range(8):
            S.op("pe", lambda kt=kt, c0=c0, cw=cw: nc.tensor.matmul(
                pb[0:1, 0:cw], lhsT=sgb[:, kt:kt + 1], rhs=wb[:, kt, c0:c0 + cw], start=(kt == 0), stop=(kt == 7)),
                reads=["sgb", "wb_%d" % ci], writes=["pb"])
        S.op("act", lambda c0=c0, cw=cw: nc.scalar.copy(out=brow[0:1, c0:c0 + cw], in_=pb[0:1, 0:cw]),
             reads=["pb"], writes=["brow"])
        S.op("pe", lambda c0=c0, cw=cw: nc.tensor.matmul(pb[:, 0:cw], lhsT=ones1[0:1, :], rhs=brow[0:1, c0:c0 + cw],
                                                         start=True, stop=True), reads=["ones1", "brow"], writes=["pb"])
        S.op("act", lambda c0=c0, cw=cw: nc.scalar.copy(out=biasb[:, c0:c0 + cw], in_=pb[:, 0:cw]),
             reads=["pb"], writes=["biasb"])

    xt = [A("xt%d" % i, [128, D], F32) for i in range(2)]
    junk = A("junk", [128, D], F32)
    xb = A("xb", [128, D], BF16)
    xT = [A("xT%d" % i, [128, 8, 128], BF16) for i in range(2)]
    ss = A("ss", [128, 1], F32)
    rs = A("rs", [128, 1], F32)
    ot = [A("ot%d" % i, [128, IN_COLS], F32) for i in range(2)]
    pT = nc.alloc_psum_tensor("pT", [128, 8, 128], BF16)
    NPP = 4
    pp = [nc.alloc_psum_tensor("pp%d" % i, [128, 512], F32) for i in range(NPP)]
    def stage_norm(t):
        b = t % 2
        S.dma("act", xt[b][:], x[t * 128:(t + 1) * 128, :], writes=["xt%d" % b], key="x%d" % b)
        S.op("act", lambda: nc.scalar.activation(out=junk[:], in_=xt[b][:], func=AF.Square, accum_out=ss[:]),
             reads=["xt%d" % b], writes=["junk", "ss"])
        S.op("dve", lambda: nc.vector.tensor_scalar(out=rs[:], in0=ss[:], scalar1=1.0 / D, scalar2=EPS,
                                                    op0=ALU.mult, op1=ALU.add), reads=["ss"], writes=["rs"])
        S.op("act", lambda: nc.scalar.activation(out=rs[:], in_=rs[:], func=AF.Sqrt), reads=["rs"], writes=["rs"])
        S.op("dve", lambda: nc.vector.reciprocal(out=rs[:], in_=rs[:]), reads=["rs"], writes=["rs"])
        S.op("act", lambda: nc.scalar.activation(out=xb[:], in_=xt[b][:], func=AF.Copy, scale=rs[:, 0:1]),
             reads=["xt%d" % b, "rs"], writes=["xb"])
        for kt in range(8):
            S.op("pe", lambda kt=kt: nc.tensor.transpose(out=pT[:, kt, :], in_=xb[:, kt * 128:(kt + 1) * 128],
                                                         identity=ident[:]), reads=["xb", "ident"], writes=["pT"])
        S.op("dve", lambda: nc.vector.tensor_copy(out=xT[b][:], in_=pT[:]), reads=["pT"], writes=["xT%d" % b])

    def stage_mm(t):
        b = t % 2
        for ci, (c0, cw) in enumerate(CG):
            pb_ = (t * len(CG) + ci) % NPP
            for kt in range(8):
                S.op("pe", lambda kt=kt: nc.tensor.matmul(
                    pp[pb_][:, 0:cw], lhsT=xT[b][:, kt, :], rhs=wb[:, kt, c0:c0 + cw], start=(kt == 0), stop=(kt == 7)),
                    reads=["xT%d" % b, "wb_%d" % ci], writes=["pp%d" % pb_])
            S.op("dve", lambda: nc.vector.tensor_tensor(
                out=ot[b][:, c0:c0 + cw], in0=pp[pb_][:, 0:cw], in1=biasb[:, c0:c0 + cw], op=ALU.add),
                reads=["pp%d" % pb_, "biasb"], writes=["ot%d_%d" % (b, ci)])
            q_ = ("sp", "pool")[ci % 2]
            S.dma(q_, proj[t * 128:(t + 1) * 128, c0:c0 + cw], ot[b][:, c0:c0 + cw], reads=["ot%d_%d" % (b, ci)], key="st%d_%d" % (b, ci))
        if True:
            gk = ["ot%d_%d" % (b, g_) for g_ in (4, 5, 6)]
            for (c_lo, dst_t, dst_d, nm_) in ((2056, qo[b], q_out, "qo%d" % b), (2568, ko[b], k_out, "ko%d" % b)):
                xv = ot[b][:, c_lo:c_lo + 512].rearrange("p (h two i) -> p h two i", h=8, two=2)
                ov = dst_t[:, :].rearrange("p (h two i) -> p h two i", h=8, two=2)
                cb = cs[:, t, :].unsqueeze(1).unsqueeze(1).to_broadcast(B4)
                sb3 = sn[:, t, :].unsqueeze(1).to_broadcast(B3)
                S.op("dve", lambda xv=xv, cb=cb: V.tensor_tensor(out=rt1[:], in0=xv, in1=cb, op=ALU.mult), reads=gk + ["cs"], writes=["rt1"])
                S.op("pool", lambda xv=xv, sb3=sb3: G.tensor_tensor(out=rt2[:, :, 0, :], in0=xv[:, :, 1, :], in1=sb3, op=ALU.mult),
                     reads=gk + ["sn"], writes=["rt2a"])
                S.op("pool", lambda xv=xv, sb3=sb3: G.tensor_tensor(out=rt2[:, :, 1, :], in0=xv[:, :, 0, :], in1=sb3, op=ALU.mult),
                     reads=gk + ["sn"], writes=["rt2b"])
                S.op("dve", lambda ov=ov: V.tensor_tensor(out=ov[:, :, 0, :], in0=rt1[:, :, 0, :], in1=rt2[:, :, 0, :], op=ALU.subtract),
                     reads=["rt1", "rt2a"], writes=[nm_ + "a"])
                S.op("dve", lambda ov=ov: V.tensor_tensor(out=ov[:, :, 1, :], in0=rt1[:, :, 1, :], in1=rt2[:, :, 1, :], op=ALU.add),
                     reads=["rt1", "rt2b"], writes=[nm_ + "b"])
                S.dma("sp", dst_d[t * 128:(t + 1) * 128, :], dst_t[:], reads=[nm_ + "a", nm_ + "b"], key=nm_)
        if True:
            S.op("act", lambda: AC.copy(out=vo[b][:], in_=ot[b][:, 3080:3592]), reads=["ot%d_6" % b, "ot%d_7" % b], writes=["vo%d" % b])
            S.dma("sp", v_out[t * 128:(t + 1) * 128, :], vo[b][:], reads=["vo%d" % b], key="vo%d" % b)

    if ntiles > 0:
        stage_norm(0)
    for t in range(ntiles):
        if t + 1 < ntiles:
            stage_norm(t + 1)
        stage_mm(t)
    S.wait_all("sp")
    return nc


def run_l1(x, c, w_mod, b_mod, norm_w, w_in, positions=None):
    nc = build_l1()
    xs = np.ascontiguousarray(x.reshape(NCORE, TOK, D))
    common = {
        "c": np.ascontiguousarray(c.reshape(8, 128).T),
        "w_mod": np.ascontiguousarray(w_mod[0]),
        "b_mod": np.ascontiguousarray(b_mod.reshape(24, 128).T),
        "norm_w": np.ascontiguousarray(norm_w.reshape(8, 128).T),
        "w_in": np.ascontiguousarray(w_in[0]),
    }
    invf = (10000.0 ** (-np.arange(32, dtype=np.float32) / 32)).astype(np.float32)
    common["invf"] = np.ascontiguousarray(np.broadcast_to(invf[None, :], (128, 32)))
    if positions is None:
        positions = np.arange(S_TOT, dtype=np.int32)[None]
    pos = np.asarray(positions).reshape(NCORE, NT, 128)
    res = run_bass_kernel_spmd(nc, [dict(common, x=xs[i], pos=np.ascontiguousarray(pos[i].T.astype(np.int32))) for i in range(NCORE)],
                               core_ids=list(range(NCORE)))
    proj = np.concatenate([r["proj"] for r in res.results], axis=0)
    modT = res.results[0]["modT"]
    qkv = tuple(np.concatenate([r[k] for r in res.results], axis=0) for k in ("q_r", "k_r", "v_b"))
    return proj, modT, qkv


AX = mybir.AxisListType.X


def build_dn(phase):
    nc = bass.Bass("TRN2", target_bir_lowering=False)
    if phase == 1:
        qkv_in = nc.dram_tensor("qkv_fm", [12, 128, TOK + 3], F32, kind="ExternalInput").ap()
        convw_in = nc.dram_tensor("convw", [128, 12, 4], F32, kind="ExternalInput").ap()
        ba_in = nc.dram_tensor("ba", [128, NT, 8], F32, kind="ExternalInput").ap()
        alog_in = nc.dram_tensor("alog", [128, 4], F32, kind="ExternalInput").ap()
        dtb_in = nc.dram_tensor("dtb", [128, 4], F32, kind="ExternalInput").ap()
        st_out = nc.dram_tensor("st_out", [128, 4, 256], F32, kind="ExternalOutput").ap()
        prod_out = nc.dram_tensor("prod", [NT, 6, 128, 512], BF16, kind="ExternalOutput").ap()
        sc_out = nc.dram_tensor("scal", [NT, 128, 8], F32, kind="ExternalOutput").ap()
    else:
        prod_in = nc.dram_tensor("prod", [NT, 6, 128, 512], BF16, kind="ExternalInput").ap()
        sc_in = nc.dram_tensor("scal", [NT, 128, 8], F32, kind="ExternalInput").ap()
        z_in = nc.dram_tensor("z", [TOK, 512], F32, kind="ExternalInput").ap()
        tt_in = nc.dram_tensor("tt_all", [8, 128, 4, 128], F32, kind="ExternalInput").ap()
        bb_in = nc.dram_tensor("bb_all", [8, 128, 4, 128], F32, kind="ExternalInput").ap()
        fl_in = nc.dram_tensor("flags", [128, 8], F32, kind="ExternalInput").ap()
        o_out = nc.dram_tensor("o_dn", [TOK, 512], F32, kind="ExternalOutput").ap()
    S = Sched(nc)
    A = nc.alloc_sbuf_tensor
    V, G, AC, PE = nc.vector, nc.gpsimd, nc.scalar, nc.tensor
    ident, identf = make_ident(nc, S)
    H4 = [128, 4, 128]

    def bc_h(ap2):
        return ap2.unsqueeze(1).to_broadcast(H4)

    def bc_c(ap2):
        return ap2.unsqueeze(2).to_broadcast(H4)

    if phase == 1:
        ltri = A("ltri", [128, 128], F32)
        strict = A("strict", [128, 128], F32)
        tril = A("tril", [128, 128], F32)
        onesf = A("onesf", [128, 128], F32)
        for nm, tl, pat, cm, cop in (("ltri", ltri, [[1, 128]], -1, ALU.is_ge), ("strict", strict, [[-1, 128]], 1, ALU.is_gt),
                                     ("tril", tril, [[-1, 128]], 1, ALU.is_ge)):
            S.op("pool", lambda tl=tl: G.memset(tl[:], 1.0), writes=[nm])
            S.op("pool", lambda tl=tl, pat=pat, cm=cm, cop=cop: G.affine_select(
                out=tl[:], in_=tl[:], pattern=pat, compare_op=cop, fill=0.0, base=0, channel_multiplier=cm),
                reads=[nm], writes=[nm])
        S.op("pool", lambda: G.memset(onesf[:], 1.0), writes=["onesf"])

        cw = A("cw", [128, 12, 4], F32)
        S.dma("sp", cw[:], convw_in[:, :, :], writes=["cw"], key="c")
        qkvT = A("qkvT", [128, 12, TOK], BF16)
        cin = [A("cin%d" % i, [128, TOK + 3], F32) for i in range(2)]
        cacc = [A("cacc0", [128, TOK], F32)] * 2
        for ct in range(12):
            b = ct % 2
            e = "dve"
            eng = V
            S.dma("sp" if b == 0 else "act", cin[b][:], qkv_in[ct, :, :], writes=["cin%d" % b], key="ci%d" % b)
            S.op(e, lambda eng=eng, b=b, ct=ct: eng.tensor_scalar(out=cacc[b][:], in0=cin[b][:, 0:TOK], scalar1=cw[:, ct, 0:1],
                                                                scalar2=None, op0=ALU.mult),
                 reads=["cin%d" % b, "cw"], writes=["cacc0"])
            for j in range(1, 4):
                S.op(e, lambda eng=eng, b=b, ct=ct, j=j: eng.scalar_tensor_tensor(
                    out=cacc[b][:], in0=cin[b][:, j:j + TOK], scalar=cw[:, ct, j:j + 1], in1=cacc[b][:],
                    op0=ALU.mult, op1=ALU.add), reads=["cin%d" % b, "cw", "cacc0"], writes=["cacc0"])
            S.op("act", lambda b=b, ct=ct: AC.activation(out=qkvT[:, ct, :], in_=cacc[b][:], func=AF.Silu),
                 reads=["cacc0"], writes=["qkvT%d" % ct])

        ba = A("ba_sb", [128, NT, 8], F32)
        al = A("al", [128, 4], F32)
        db = A("db", [128, 4], F32)
        S.dma("sp", ba[:], ba_in[:, :, :], writes=["ba"], key="c")
        S.dma("sp", al[:], alog_in[:, :], writes=["al"], key="c")
        S.dma("sp", db[:], dtb_in[:, :], writes=["db"], key="c")
        G3 = [128, NT, 4]
        xa = A("xa", G3, F32); axa = A("axa", G3, F32); rl = A("rl", G3, F32)
        gg = A("gg", G3, F32); beta = A("beta", G3, F32); nbeta = A("nbeta", G3, F32); ea = A("ea", [128, 4], F32)
        S.op("dve", lambda: V.tensor_tensor(out=xa[:], in0=ba[:, :, 4:8], in1=db[:, :].unsqueeze(1).to_broadcast(G3), op=ALU.add),
             reads=["ba", "db"], writes=["xa"])
        S.op("act", lambda: AC.activation(out=axa[:], in_=xa[:], func=AF.Abs), reads=["xa"], writes=["axa"])
        S.op("act", lambda: AC.activation(out=axa[:], in_=axa[:], func=AF.Exp, scale=-1.0), reads=["axa"], writes=["axa"])
        S.op("dve", lambda: V.tensor_scalar(out=axa[:], in0=axa[:], scalar1=1.0, scalar2=None, op0=ALU.add), reads=["axa"], writes=["axa"])
        S.op("act", lambda: AC.activation(out=axa[:], in_=axa[:], func=AF.Ln), reads=["axa"], writes=["axa"])
        S.op("dve", lambda: V.tensor_single_scalar(out=rl[:], in_=xa[:], scalar=0.0, op=ALU.max), reads=["xa"], writes=["rl"])
        S.op("dve", lambda: V.tensor_tensor(out=rl[:], in0=rl[:], in1=axa[:], op=ALU.add), reads=["rl", "axa"], writes=["rl"])
        S.op("act", lambda: AC.activation(out=ea[:], in_=al[:], func=AF.Exp), reads=["al"], writes=["ea"])
        S.op("dve", lambda: V.scalar_tensor_tensor(out=gg[:], in0=rl[:], scalar=-1.0, in1=ea[:, :].unsqueeze(1).to_broadcast(G3),
                                                   op0=ALU.mult, op1=ALU.mult), reads=["rl", "ea"], writes=["gg"])
        S.op("act", lambda: AC.activation(out=beta[:], in_=ba[:, :, 0:4], func=AF.Sigmoid), reads=["ba"], writes=["beta"])
        S.op("dve", lambda: V.tensor_scalar(out=nbeta[:], in0=beta[:], scalar1=-1.0, scalar2=None, op0=ALU.mult),
             reads=["beta"], writes=["nbeta"])

    pf = [nc.alloc_psum_tensor("pf%d" % i, [128, 512], F32) for i in range(4)]
    pbf = [nc.alloc_psum_tensor("pbf%d" % i, [128, 1024], BF16) for i in range(2)]
    pS = nc.alloc_psum_tensor("pS", [128, 4, 256], F32)
    rr = {"pf": 0, "pbf": 0}

    def npf():
        i = rr["pf"]; rr["pf"] = (i + 1) % 4
        return pf[i], "pf%d" % i

    def npbf():
        i = rr["pbf"]; rr["pbf"] = (i + 1) % 2
        return pbf[i], "pbf%d" % i

    def v4(p):
        return p[:, 0:512].rearrange("p (h c) -> p h c", h=4)

    def mm4(lhs, lk, rhs, rk, n=128, rhs_shared=False, lhs_shared=False):
        p, pk = npf()
        for h in range(4):
            l = lhs if lhs_shared else lhs(h)
            r = rhs if rhs_shared else rhs(h)
            S.op("pe", lambda l=l, r=r, h=h: PE.matmul(p[:, h * 128:h * 128 + n], lhsT=l, rhs=r, start=True, stop=True),
                 reads=[lk, rk], writes=[pk])
        return p, pk

    def tr4(src, sk):
        p, pk = npbf()
        for h in range(4):
            S.op("pe", lambda h=h: PE.transpose(out=p[:, h * 128:(h + 1) * 128], in_=src(h), identity=ident[:]),
                 reads=[sk, "ident"], writes=[pk])
        return p[:, 0:512].rearrange("p (h c) -> p h c", h=4), pk

    def tr4f(src, sk):
        p, pk = npf()
        for h in range(4):
            S.op("pe", lambda h=h: PE.transpose(out=p[:, h * 128:(h + 1) * 128], in_=src(h), identity=identf[:]),
                 reads=[sk, "identf"], writes=[pk])
        return v4(p), pk

    NAMES = ("G1", "gB", "dec", "decT", "decS", "Ee", "qd", "sq", "kn", "knT", "vb", "Pb", "Qb", "Rb", "Xb", "kbg", "kd", "Wm", "WT",
             "nMT", "attn", "attnT", "UA", "scq", "rq", "gl", "gcs", "egc", "ekd", "bege", "ssk", "rk", "ssq")

    def alloc_set(x):
        def T(name, dt=BF16, shape=H4):
            return A(name + x, shape, dt)
        L = {}
        for nm in ("G1", "gB", "dec", "decT", "decS", "Ee", "sq"):
            L[nm] = T(nm, F32)
        for nm in ("qd", "kn", "knT", "vb", "Xb", "kbg", "kd", "Wm", "WT", "nMT", "attn", "attnT"):
            L[nm] = T(nm)
        L["Pb"] = [T("P0", F32), T("P1", F32)]; L["Qb"] = [T("Q0", F32), T("Q1", F32)]; L["Rb"] = [T("R0", F32), T("R1", F32)]
        L["UA"] = A("UA" + x, [128, 4, 256], BF16)
        L["scq"] = A("scq" + x, [128, 8], F32); L["rq"] = L["scq"][:, 0:4]; L["gl"] = L["scq"][:, 4:8]
        L["gcs"] = A("gcs" + x, [128, 8], F32)
        for nm in ("egc", "ekd", "bege", "ssk", "rk", "ssq"):
            L[nm] = A(nm + x, [128, 4], F32)
        return L
    if phase == 1:
        SETS = [alloc_set("_a"), alloc_set("_b")]
    else:
        vnew = A("vnew", H4, BF16); osb = A("osb", H4, F32); yb = A("yb", H4, F32); sq = A("sq", H4, F32)
        sso = A("sso", [128, 4], F32); ro = A("ro", [128, 4], F32)
    STf = A("STf", [128, 4, 256], F32); STb = A("STb", [128, 4, 256], BF16)
    if phase == 1:
        for L_, x_ in zip(SETS, ("_a", "_b")):
            S.op("pool", lambda L_=L_: G.memset(L_["UA"][:], 0.0), writes=["UA" + x_])
    S.op("pool", lambda: G.memset(STf[:], 0.0), writes=["STf"])
    Wd = 256 if phase == 1 else 128

    if phase == 1:
        for h in range(4):
            S.op("pool", lambda h=h: G.tensor_copy(out=STf[:, h, 128:256], in_=identf[:]), reads=["identf", "STf"], writes=["STf"])
    else:
        fl = A("fl", [128, 8], F32)
        S.dma("sp", fl[:], fl_in[:, :], writes=["fl"], key="c")
        ttf = A("ttf", H4, F32); ttb = A("ttb", H4, BF16); bbf = A("bbf", H4, F32); tmpc = A("tmpc", H4, F32)
        S.op("act", lambda: AC.copy(out=STb[:], in_=STf[:]), reads=["STf"], writes=["STb"])
        for j in range(7):
            S.dma("sp", ttf[:], tt_in[j, :, :, :], writes=["ttf"], key="cc")
            S.dma("act", bbf[:], bb_in[j, :, :, :], writes=["bbf"], key="cc2")
            S.op("dve", lambda: V.tensor_copy(out=ttb[:], in_=ttf[:]), reads=["ttf"], writes=["ttb"])
            p, pk = mm4(lambda h: ttb[:, h, :], "ttb", lambda h: STb[:, h, 0:128], "STb")
            S.op("dve", lambda p=p: V.tensor_tensor(out=tmpc[:], in0=v4(p), in1=bbf[:], op=ALU.add), reads=[pk, "bbf"], writes=["tmpc"])
            S.op("dve", lambda: V.tensor_tensor(out=tmpc[:], in0=tmpc[:], in1=STf[:, :, 0:128], op=ALU.subtract),
                 reads=["tmpc", "STf"], writes=["tmpc"])
            S.op("dve", lambda j=j: V.scalar_tensor_tensor(out=STf[:, :, 0:128], in0=tmpc[:], scalar=fl[:, j:j + 1], in1=STf[:, :, 0:128],
                                                         op0=ALU.mult, op1=ALU.add), reads=["tmpc", "fl", "STf"], writes=["STf"])
            S.op("act", lambda: AC.copy(out=STb[:], in_=STf[:]), reads=["STf"], writes=["STb"])
        zt = A("zt", [128, 512], F32); sz = A("sz", [128, 512], F32)
        PR = [A("PR%d" % i, [128, 6, 512], BF16) for i in range(2)]; SCL = [A("SCL%d" % i, [128, 8], F32) for i in range(2)]

        def load_prod(t):
            b = t % 2
            S.dma("sp" if b == 0 else "act", PR[b][:], prod_in[t].rearrange("k p f -> p k f"), writes=["PR%d" % b], key="pr%d" % b)
            S.dma("sp", SCL[b][:], sc_in[t], writes=["SCL%d" % b], key="sc%d" % b)
        load_prod(0)
    if phase == 1:
        S.op("act", lambda: AC.copy(out=STb[:], in_=STf[:]), reads=["STf"], writes=["STb"])

    QK_ALL = ["qkvT%d" % i for i in range(12)]

    def tile_gen(t, L):
        (G1, gB, dec, decT, decS, Ee, qd, sq, kn, knT, vb, Pb, Qb, Rb, Xb, kbg, kd, Wm, WT,
         nMT, attn, attnT, UA, scq, rq, gl, gcs, egc, ekd, bege, ssk, rk, ssq) = (L[k] for k in NAMES)
        ts = slice(t * 128, (t + 1) * 128)
        g_t = gg[:, t, :]
        S.op("pool", lambda: G.tensor_tensor(out=G1[:], in0=bc_h(ltri[:, :]), in1=bc_c(g_t), op=ALU.mult), reads=["ltri", "gg"], writes=["G1"])
        S.op("pool", lambda: G.tensor_tensor(out=gB[:], in0=bc_h(onesf[:, :]), in1=bc_c(g_t), op=ALU.mult), reads=["onesf", "gg"], writes=["gB"])
        yield
        p, pk = mm4(lambda h: G1[:, h, :], "G1", strict[:, :], "strict", rhs_shared=True)
        S.op("act", lambda p=p: AC.activation(out=dec[:], in_=v4(p), func=AF.Exp), reads=[pk], writes=["dec"])
        S.op("pool", lambda: G.tensor_tensor(out=decT[:], in0=dec[:], in1=bc_h(tril[:, :]), op=ALU.mult), reads=["dec", "tril"], writes=["decT"])
        S.op("pool", lambda: G.tensor_tensor(out=decS[:], in0=dec[:], in1=bc_h(strict[:, :]), op=ALU.mult), reads=["dec", "strict"], writes=["decS"])
        yield
        p, pk = npf()
        S.op("pe", lambda p=p: PE.matmul(p[:, 0:4], lhsT=ltri[:, :], rhs=g_t, start=True, stop=True), reads=["ltri", "gg"], writes=[pk])
        S.op("pe", lambda p=p: PE.matmul(p[:, 4:8], lhsT=onesf[:, :], rhs=g_t, start=True, stop=True), reads=["onesf", "gg"], writes=[pk])
        S.op("dve", lambda p=p: V.tensor_copy(out=gcs[:], in_=p[:, 0:8]), reads=[pk], writes=["gcs"])
        S.op("act", lambda: AC.activation(out=egc[:], in_=gcs[:, 0:4], func=AF.Exp), reads=["gcs"], writes=["egc"])
        S.op("act", lambda: AC.activation(out=gl, in_=gcs[:, 4:8], func=AF.Exp), reads=["gcs"], writes=["gl"])
        S.op("dve", lambda: V.tensor_tensor(out=ekd[:], in0=gcs[:, 4:8], in1=gcs[:, 0:4], op=ALU.subtract), reads=["gcs"], writes=["ekd"])
        S.op("act", lambda: AC.activation(out=ekd[:], in_=ekd[:], func=AF.Exp), reads=["ekd"], writes=["ekd"])
        S.op("dve", lambda: V.tensor_tensor(out=bege[:], in0=beta[:, t, :], in1=egc[:], op=ALU.mult), reads=["beta", "egc"], writes=["bege"])
        yield
        p, pk = mm4(lambda h: gB[:, h, :], "gB", ltri[:, :], "ltri", rhs_shared=True)
        S.op("act", lambda p=p: AC.activation(out=Ee[:], in_=v4(p), func=AF.Exp), reads=[pk], writes=["Ee"])
        S.op("dve", lambda: V.tensor_tensor(out=qd[:], in0=qkvT[:, 0:4, ts], in1=Ee[:], op=ALU.mult), reads=QK_ALL[0:4] + ["Ee"], writes=["qd"])
        yield
        pv_, pk = tr4(lambda h: qkvT[:, 4 + h, ts], "qkvT4")
        for kk in QK_ALL[4:8]:
            S.bufs.setdefault(kk, {"w": None, "r": []})
        S.op("act", lambda pv_=pv_: AC.activation(out=sq[:], in_=pv_, func=AF.Square), reads=[pk] + QK_ALL[4:8], writes=["sq"])
        S.op("dve", lambda: V.tensor_reduce(out=ssk[:], in_=sq[:], axis=AX, op=ALU.add), reads=["sq"], writes=["ssk"])
        S.op("dve", lambda: V.tensor_scalar(out=ssk[:], in0=ssk[:], scalar1=EPS, scalar2=None, op0=ALU.add), reads=["ssk"], writes=["ssk"])
        S.op("act", lambda: AC.activation(out=ssk[:], in_=ssk[:], func=AF.Sqrt), reads=["ssk"], writes=["ssk"])
        S.op("dve", lambda: V.reciprocal(out=rk[:], in_=ssk[:]), reads=["ssk"], writes=["rk"])
        S.op("dve", lambda pv_=pv_: V.tensor_tensor(out=kn[:], in0=pv_, in1=bc_c(rk[:, :]), op=ALU.mult), reads=[pk, "rk"], writes=["kn"])
        yield
        pv_, pk = tr4(lambda h: kn[:, h, :], "kn")
        S.op("act", lambda pv_=pv_: AC.copy(out=knT[:], in_=pv_), reads=[pk], writes=["knT"])
        yield
        pv_, pk = tr4(lambda h: qkvT[:, 8 + h, ts], "qkvT8")
        S.op("dve", lambda pv_=pv_: V.tensor_tensor(out=vb[:], in0=pv_, in1=bc_c(beta[:, t, :]), op=ALU.mult),
             reads=[pk, "beta"] + QK_ALL[8:12], writes=["vb"])
        yield
        pv_, pk = tr4(lambda h: qkvT[:, h, ts], "qkvT0")
        S.op("act", lambda pv_=pv_: AC.activation(out=sq[:], in_=pv_, func=AF.Square), reads=[pk] + QK_ALL[0:4], writes=["sq"])
        S.op("dve", lambda: V.tensor_reduce(out=ssq[:], in_=sq[:], axis=AX, op=ALU.add), reads=["sq"], writes=["ssq"])
        S.op("dve", lambda: V.tensor_scalar(out=ssq[:], in0=ssq[:], scalar1=EPS, scalar2=128.0, op0=ALU.add, op1=ALU.mult),
             reads=["ssq"], writes=["ssq"])
        S.op("act", lambda: AC.activation(out=ssq[:], in_=ssq[:], func=AF.Sqrt), reads=["ssq"], writes=["ssq"])
        S.op("dve", lambda: V.reciprocal(out=rq, in_=ssq[:]), reads=["ssq"], writes=["rq"])
        yield
        p, pk = mm4(lambda h: knT[:, h, :], "knT", lambda h: knT[:, h, :], "knT")
        for h in range(4):
            S.op("dve", lambda p=p, h=h: V.scalar_tensor_tensor(out=Pb[0][:, h, :], in0=p[:, h * 128:(h + 1) * 128],
                                                               scalar=nbeta[:, t, h:h + 1], in1=decS[:, h, :], op0=ALU.mult, op1=ALU.mult),
                 reads=[pk, "nbeta", "decS"], writes=["P0"])
        yield
        pv_, pk = tr4f(lambda h: Pb[0][:, h, :], "P0")
        S.op("act", lambda pv_=pv_: AC.copy(out=Qb[0][:], in_=pv_), reads=[pk], writes=["Q0"])
        S.op("pool", lambda: G.tensor_tensor(out=Rb[0][:], in0=Qb[0][:], in1=bc_h(identf[:, :]), op=ALU.add), reads=["Q0", "identf"], writes=["R0"])
        cp, cq, cr = 0, 0, 0
        for k in range(6):
            np_, nq, nr = 1 - cp, 1 - cq, 1 - cr
            yield
            p1, pk1 = mm4(lambda h: Qb[cq][:, h, :], "Q%d" % cq, lambda h: Pb[cp][:, h, :], "P%d" % cp)
            if k < 5:
                yield
                p2, pk2 = mm4(lambda h: Pb[cp][:, h, :], "P%d" % cp, lambda h: Qb[cq][:, h, :], "Q%d" % cq)
            S.op("act", lambda p1=p1, np_=np_: AC.copy(out=Pb[np_][:], in_=v4(p1)), reads=[pk1], writes=["P%d" % np_])
            if k < 5:
                S.op("dve", lambda p2=p2, nq=nq: V.tensor_copy(out=Qb[nq][:], in_=v4(p2)), reads=[pk2], writes=["Q%d" % nq])
            yield
            p3, pk3 = mm4(lambda h: Pb[np_][:, h, :], "P%d" % np_, lambda h: Rb[cr][:, h, :], "R%d" % cr)
            S.op("dve", lambda p3=p3, nr=nr, cr=cr: V.tensor_tensor(out=Rb[nr][:], in0=v4(p3), in1=Rb[cr][:], op=ALU.add),
                 reads=[pk3, "R%d" % cr], writes=["R%d" % nr])
            cp, cq, cr = np_, nq, nr
        S.op("act", lambda cr=cr: AC.copy(out=Xb[:], in_=Rb[cr][:]), reads=["R%d" % cr], writes=["Xb"])
        X = Xb; xk = "Xb"
        S.op("pool", lambda: G.tensor_tensor(out=kbg[:], in0=kn[:], in1=bc_c(bege[:, :]), op=ALU.mult), reads=["kn", "bege"], writes=["kbg"])
        S.op("pool", lambda: G.tensor_tensor(out=kd[:], in0=kn[:], in1=bc_c(ekd[:, :]), op=ALU.mult), reads=["kn", "ekd"], writes=["kd"])
        yield
        p, pk = mm4(lambda h: X[:, h, :], xk, lambda h: kbg[:, h, :], "kbg")
        S.op("act", lambda p=p: AC.copy(out=Wm[:], in_=v4(p)), reads=[pk], writes=["Wm"])
        yield
        p, pk = mm4(lambda h: X[:, h, :], xk, lambda h: vb[:, h, :], "vb")
        S.op("dve", lambda p=p: V.tensor_copy(out=UA[:, :, 0:128], in_=v4(p)), reads=[pk], writes=["UA"])
        yield
        p, pk = mm4(lambda h: Wm[:, h, :], "Wm", lambda h: kd[:, h, :], "kd")
        S.op("act", lambda p=p: AC.mul(out=nMT[:], in_=v4(p), mul=-1.0), reads=[pk], writes=["nMT"])
        yield
        p, pk = mm4(lambda h: kbg[:, h, :], "kbg", lambda h: X[:, h, :], xk)
        S.op("act", lambda p=p: AC.copy(out=WT[:], in_=v4(p)), reads=[pk], writes=["WT"])
        yield
        p, pk = mm4(lambda h: qkvT[:, h, ts], "qkvT0", lambda h: knT[:, h, :], "knT")
        S.op("dve", lambda p=p: V.tensor_tensor(out=attn[:], in0=v4(p), in1=decT[:], op=ALU.mult), reads=[pk, "decT"] + QK_ALL[0:4], writes=["attn"])
        yield
        pv_, pk = tr4(lambda h: attn[:, h, :], "attn")
        S.op("act", lambda pv_=pv_: AC.copy(out=attnT[:], in_=pv_), reads=[pk], writes=["attnT"])
        for kx, (tl, tk) in enumerate(((WT, "WT"), (None, "UA"), (qd, "qd"), (attnT, "attnT"), (nMT, "nMT"), (kd, "kd"))):
            src = UA[:, :, 0:128] if tl is None else tl[:]
            S.dma("pool", prod_out[t, kx].rearrange("p (h c) -> p h c", h=4), src, reads=[tk], key="sp%d" % kx)
        S.dma("pool", sc_out[t], scq[:], reads=["rq", "gl"], key="spsc")
        WTv, Uv, qdv, attnTv, nMTv, kdv, rqv, glv = WT, UA, qd, attnT, nMT, kd, rq, gl
        kWT, kU, kqd, kat, knm, kkd, krq, kgl = "WT", "UA", "qd", "attnT", "nMT", "kd", "rq", "gl"
        yield
        for h in range(4):
            S.op("pe", lambda h=h: PE.matmul(pS[:, h, 0:Wd], lhsT=kdv[:, h, :], rhs=Uv[:, h, 0:Wd], start=True, stop=False),
                 reads=[kkd, kU], writes=["pS"])
            S.op("pe", lambda h=h: PE.matmul(pS[:, h, 0:Wd], lhsT=nMTv[:, h, :], rhs=STb[:, h, 0:Wd], start=False, stop=True),
                 reads=[knm, "STb"], writes=["pS"])
        for h in range(4):
            S.op("dve", lambda h=h: V.scalar_tensor_tensor(out=STf[:, h, 0:Wd], in0=STf[:, h, 0:Wd], scalar=glv[:, h:h + 1], in1=pS[:, h, 0:Wd],
                                                         op0=ALU.mult, op1=ALU.add), reads=["STf", kgl, "pS"], writes=["STf"])
        S.op("act", lambda: AC.copy(out=STb[:, :, 0:Wd], in_=STf[:, :, 0:Wd]), reads=["STf"], writes=["STb"])

    if phase == 1:
        S.shared = set(S.bufs.keys()) | {"pf%d" % i for i in range(4)} | {"pbf0", "pbf1", "pS", "STf", "STb"}
        for t0 in range(0, NT, 2):
            gens = [(tile_gen(t0, SETS[0]), "_a"), (tile_gen(t0 + 1, SETS[1]), "_b")]
            while gens:
                for g_, ns_ in list(gens):
                    S.ns = ns_
                    try:
                        next(g_)
                    except StopIteration:
                        gens.remove((g_, ns_))
            S.ns = ""
    for t in (range(NT) if phase == 2 else ()):
        ts = slice(t * 128, (t + 1) * 128)
        b = t % 2
        if t + 1 < NT:
            load_prod(t + 1)
        pv6 = PR[b][:, :, :].rearrange("p k (h c) -> p k h c", h=4)
        WTv, Uv, qdv, attnTv, nMTv, kdv = (pv6[:, kx] for kx in range(6))
        rqv, glv = SCL[b][:, 0:4], SCL[b][:, 4:8]
        kWT = kU = kqd = kat = knm = kkd = "PR%d" % b
        krq = kgl = "SCL%d" % b
        p, pk = mm4(lambda h: WTv[:, h, :], kWT, lambda h: STb[:, h, 0:128], "STb")
        S.op("dve", lambda p=p: V.tensor_tensor(out=vnew[:], in0=Uv[:, :, 0:128], in1=v4(p), op=ALU.subtract), reads=[pk, kU], writes=["vnew"])
        p, pk = npf()
        for h in range(4):
            S.op("pe", lambda p=p, h=h: PE.matmul(p[:, h * 128:(h + 1) * 128], lhsT=qdv[:, h, :], rhs=STb[:, h, 0:128], start=True, stop=False),
                 reads=[kqd, "STb"], writes=[pk])
            S.op("pe", lambda p=p, h=h: PE.matmul(p[:, h * 128:(h + 1) * 128], lhsT=attnTv[:, h, :], rhs=vnew[:, h, :], start=False, stop=True),
                 reads=[kat, "vnew"], writes=[pk])
        S.op("dve", lambda p=p: V.tensor_tensor(out=osb[:], in0=v4(p), in1=bc_c(rqv), op=ALU.mult), reads=[pk, krq], writes=["osb"])
        S.op("act", lambda: AC.activation(out=sq[:], in_=osb[:], func=AF.Square), reads=["osb"], writes=["sq"])
        S.op("dve", lambda: V.tensor_reduce(out=sso[:], in_=sq[:], axis=AX, op=ALU.add), reads=["sq"], writes=["sso"])
        S.op("dve", lambda: V.tensor_scalar(out=sso[:], in0=sso[:], scalar1=1.0 / 128, scalar2=EPS, op0=ALU.mult, op1=ALU.add),
             reads=["sso"], writes=["sso"])
        S.op("act", lambda: AC.activation(out=sso[:], in_=sso[:], func=AF.Sqrt), reads=["sso"], writes=["sso"])
        S.op("dve", lambda: V.reciprocal(out=ro[:], in_=sso[:]), reads=["sso"], writes=["ro"])
        S.dma("sp", zt[:], z_in[ts, :], writes=["zt"], key="z")
        S.op("act", lambda: AC.activation(out=sz[:], in_=zt[:], func=AF.Silu), reads=["zt"], writes=["sz"])
        S.op("dve", lambda: V.tensor_tensor(out=yb[:], in0=osb[:], in1=bc_c(ro[:, :]), op=ALU.mult), reads=["osb", "ro"], writes=["yb"])
        S.op("pool", lambda: G.tensor_tensor(out=yb[:], in0=yb[:], in1=sz[:, :].rearrange("p (h c) -> p h c", h=4), op=ALU.mult),
             reads=["yb", "sz"], writes=["yb"])
        S.dma("pool", o_out[ts, :], yb[:].rearrange("p h c -> p (h c)"), reads=["yb"], key="st")
        for h in range(4):
            S.op("pe", lambda h=h: PE.matmul(pS[:, h, 0:Wd], lhsT=kdv[:, h, :], rhs=Uv[:, h, 0:Wd], start=True, stop=False),
                 reads=[kkd, kU], writes=["pS"])
            S.op("pe", lambda h=h: PE.matmul(pS[:, h, 0:Wd], lhsT=nMTv[:, h, :], rhs=STb[:, h, 0:Wd], start=False, stop=True),
                 reads=[knm, "STb"], writes=["pS"])
        for h in range(4):
            S.op("dve", lambda h=h: V.scalar_tensor_tensor(out=STf[:, h, 0:Wd], in0=STf[:, h, 0:Wd], scalar=glv[:, h:h + 1], in1=pS[:, h, 0:Wd],
                                                         op0=ALU.mult, op1=ALU.add), reads=["STf", kgl, "pS"], writes=["STf"])
        S.op("act", lambda: AC.copy(out=STb[:, :, 0:Wd], in_=STf[:, :, 0:Wd]), reads=["STf"], writes=["STb"])
    if phase == 1:
        S.dma("sp", st_out[:, :, :], STf[:], reads=["STf"], key="st")
    S.wait_all("sp")
    return nc


def dn_inputs(proj, conv_w, a_log, dt_bias):
    qkv = proj[:, 0:1536]
    pad = np.concatenate([np.zeros((3, 1536), np.float32), qkv], axis=0)
    maps = []
    for i in range(NCORE):
        seg = pad[i * TOK:(i + 1) * TOK + 3]
        m = {
            "qkv_fm": np.ascontiguousarray(seg.T.reshape(12, 128, TOK + 3)),
            "convw": np.ascontiguousarray(conv_w[0].T.reshape(12, 128, 4).transpose(1, 0, 2)),
            "ba": np.ascontiguousarray(proj[i * TOK:(i + 1) * TOK, 2048:2056].reshape(NT, 128, 8).transpose(1, 0, 2)),
            "alog": np.ascontiguousarray(np.broadcast_to(a_log[0][None, :], (128, 4))),
            "dtb": np.ascontiguousarray(np.broadcast_to(dt_bias[0][None, :], (128, 4))),
        }
        maps.append(m)
    return maps


def run_dn(proj, conv_w, a_log, dt_bias):
    maps = dn_inputs(proj, conv_w, a_log, dt_bias)
    r1 = run_bass_kernel_spmd(build_dn(1), maps, core_ids=list(range(NCORE)))
    st = np.stack([r["st_out"] for r in r1.results], axis=0)
    bb_all = np.ascontiguousarray(st[:, :, :, 0:128])
    tt_all = np.ascontiguousarray(st[:, :, :, 128:256].transpose(0, 3, 2, 1))
    maps2 = []
    for i in range(NCORE):
        fl = np.zeros((128, 8), np.float32)
        fl[:, :i] = 1.0
        maps2.append({"z": np.ascontiguousarray(proj[i * TOK:(i + 1) * TOK, 1536:2048]), "tt_all": tt_all, "bb_all": bb_all, "flags": fl,
                      "prod": r1.results[i]["prod"], "scal": r1.results[i]["scal"]})
    r2 = run_bass_kernel_spmd(build_dn(2), maps2, core_ids=list(range(NCORE)))
    return np.concatenate([r["o_dn"] for r in r2.results], axis=0), st


def build_rope():
    nc = bass.Bass("TRN2", target_bir_lowering=False)
    q_in = nc.dram_tensor("q", [TOK, 512], F32, kind="ExternalInput").ap()
    k_in = nc.dram_tensor("k", [TOK, 512], F32, kind="ExternalInput").ap()
    pos_in = nc.dram_tensor("pos", [128, NT], mybir.dt.int32, kind="ExternalInput").ap()
    invf_in = nc.dram_tensor("invf", [128, 32], F32, kind="ExternalInput").ap()
    v_in = nc.dram_tensor("v", [TOK, 512], F32, kind="ExternalInput").ap()
    q_out = nc.dram_tensor("q_r", [TOK, 512], BF16, kind="ExternalOutput").ap()
    k_out = nc.dram_tensor("k_r", [TOK, 512], BF16, kind="ExternalOutput").ap()
    v_out = nc.dram_tensor("v_b", [TOK, 512], BF16, kind="ExternalOutput").ap()
    S = Sched(nc)
    A = nc.alloc_sbuf_tensor
    V, G, AC = nc.vector, nc.gpsimd, nc.scalar
    posi = A("posi", [128, NT], mybir.dt.int32); posf = A("posf", [128, NT], F32); invf = A("invf_sb", [128, 32], F32)
    A3 = [128, NT, 32]
    ang = A("ang", A3, F32); sn = A("sn", A3, F32); cs = A("cs", A3, F32); tmp = A("tmpa", A3, F32)
    S.dma("sp", posi[:], pos_in[:, :], writes=["posi"], key="c")
    S.dma("sp", invf[:], invf_in[:, :], writes=["invf"], key="c")
    S.op("dve", lambda: V.tensor_copy(out=posf[:], in_=posi[:]), reads=["posi"], writes=["posf"])
    S.op("dve", lambda: V.tensor_tensor(out=ang[:], in0=posf[:, :].unsqueeze(2).to_broadcast(A3),
                                        in1=invf[:, :].unsqueeze(1).to_broadcast(A3), op=ALU.mult), reads=["posf", "invf"], writes=["ang"])
    ki = A("ki", A3, mybir.dt.int32); kf = A("kf", A3, F32); xs = A("xs", A3, F32); stp = A("stp", A3, F32)
    for dst, off, nm in ((sn, 0.0, "sn"), (cs, 0.5 * PI, "cs")):
        S.op("dve", lambda off=off: V.tensor_scalar(out=xs[:], in0=ang[:], scalar1=off, scalar2=None, op0=ALU.add), reads=["ang"], writes=["xs"])
        S.op("dve", lambda: V.tensor_scalar(out=tmp[:], in0=xs[:], scalar1=1.0 / (2 * PI), scalar2=None, op0=ALU.mult), reads=["xs"], writes=["tmpa"])
        S.op("dve", lambda: V.tensor_copy(out=ki[:], in_=tmp[:]), reads=["tmpa"], writes=["ki"])
        S.op("dve", lambda: V.tensor_copy(out=kf[:], in_=ki[:]), reads=["ki"], writes=["kf"])
        S.op("dve", lambda: V.scalar_tensor_tensor(out=tmp[:], in0=kf[:], scalar=-2 * PI, in1=xs[:], op0=ALU.mult, op1=ALU.add),
             reads=["kf", "xs"], writes=["tmpa"])
        S.op("dve", lambda: V.tensor_scalar(out=stp[:], in0=tmp[:], scalar1=-PI, scalar2=1e30, op0=ALU.add, op1=ALU.mult), reads=["tmpa"], writes=["stp"])
        S.op("dve", lambda: V.tensor_scalar(out=stp[:], in0=stp[:], scalar1=0.0, scalar2=1.0, op0=ALU.max, op1=ALU.min), reads=["stp"], writes=["stp"])
        S.op("dve", lambda: V.scalar_tensor_tensor(out=tmp[:], in0=stp[:], scalar=-2 * PI, in1=tmp[:], op0=ALU.mult, op1=ALU.add),
             reads=["stp", "tmpa"], writes=["tmpa"])
        S.op("dve", lambda: V.tensor_scalar(out=tmp[:], in0=tmp[:], scalar1=-PI, scalar2=PI, op0=ALU.max, op1=ALU.min), reads=["tmpa"], writes=["tmpa"])
        S.op("act", lambda dst=dst: AC.activation(out=dst[:], in_=tmp[:], func=AF.Sin), reads=["tmpa"], writes=[nm])
    xt = [A("xr%d" % i, [128, 512], F32) for i in range(2)]
    t1 = A("t1", [128, 8, 2, 32], F32); t2 = A("t2", [128, 8, 2, 32], F32); ot = [A("or%d" % i, [128, 512], BF16) for i in range(2)]
    B4 = [128, 8, 2, 32]; B3 = [128, 8, 32]
    it = 0
    for src, dstd in ((q_in, q_out), (k_in, k_out)):
        for t in range(NT):
            b = it % 2; it += 1
            ts = slice(t * 128, (t + 1) * 128)
            S.dma("sp" if b == 0 else "act", xt[b][:], src[ts, :], writes=["xr%d" % b], key="x%d" % b)
            xv = xt[b][:, :].rearrange("p (h two i) -> p h two i", h=8, two=2)
            cb = cs[:, t, :].unsqueeze(1).unsqueeze(1).to_broadcast(B4)
            S.op("dve", lambda xv=xv, cb=cb: V.tensor_tensor(out=t1[:], in0=xv, in1=cb, op=ALU.mult), reads=["xr%d" % b, "cs"], writes=["t1"])
            sb3 = sn[:, t, :].unsqueeze(1).to_broadcast(B3)
            S.op("pool", lambda xv=xv, sb3=sb3: G.tensor_tensor(out=t2[:, :, 0, :], in0=xv[:, :, 1, :], in1=sb3, op=ALU.mult),
                 reads=["xr%d" % b, "sn"], writes=["t2a"])
            S.op("pool", lambda xv=xv, sb3=sb3: G.tensor_tensor(out=t2[:, :, 1, :], in0=xv[:, :, 0, :], in1=sb3, op=ALU.mult),
                 reads=["xr%d" % b, "sn"], writes=["t2b"])
            ov = ot[b][:, :].rearrange("p (h two i) -> p h two i", h=8, two=2)
            S.op("dve", lambda ov=ov: V.tensor_tensor(out=ov[:, :, 0, :], in0=t1[:, :, 0, :], in1=t2[:, :, 0, :], op=ALU.subtract),
                 reads=["t1", "t2a"], writes=["or%da" % b])
            S.op("dve", lambda ov=ov: V.tensor_tensor(out=ov[:, :, 1, :], in0=t1[:, :, 1, :], in1=t2[:, :, 1, :], op=ALU.add),
                 reads=["t1", "t2b"], writes=["or%db" % b])
            S.dma("sp", dstd[ts, :], ot[b][:], reads=["or%da" % b, "or%db" % b], key="st%d" % b)
    vt = [A("vt%d" % i, [128, 512], F32) for i in range(2)]; vo = [A("vo%d" % i, [128, 512], BF16) for i in range(2)]
    for t in range(NT):
        b = t % 2
        ts = slice(t * 128, (t + 1) * 128)
        S.dma("act", vt[b][:], v_in[ts, :], writes=["vt%d" % b], key="v%d" % b)
        S.op("act", lambda b=b: AC.copy(out=vo[b][:], in_=vt[b][:]), reads=["vt%d" % b], writes=["vo%d" % b])
        S.dma("sp", v_out[ts, :], vo[b][:], reads=["vo%d" % b], key="sv%d" % b)
    S.wait_all("sp")
    return nc


NBLK = 16
NEG = -30000.0


def build_attn():
    nc = bass.Bass("TRN2", target_bir_lowering=False)
    NB = 3 * NBLK
    qT_in = nc.dram_tensor("qT", [NB, 4, 128, 128], BF16, kind="ExternalInput").ap()
    kT_in = nc.dram_tensor("kT", [NB, 4, 128, 256], BF16, kind="ExternalInput").ap()
    v_in = nc.dram_tensor("v", [NB, 2, 128, 512], BF16, kind="ExternalInput").ap()
    fl_in = nc.dram_tensor("bflag", [128, NB], F32, kind="ExternalInput").ap()
    o_out = nc.dram_tensor("o_un", [NB, 128, 512], F32, kind="ExternalOutput").ap()
    l_out = nc.dram_tensor("l_un", [NB, 128, 8], F32, kind="ExternalOutput").ap()
    m_out = nc.dram_tensor("m_un", [NB, 128, 8], F32, kind="ExternalOutput").ap()
    S = Sched(nc)
    A = nc.alloc_sbuf_tensor
    V, G, AC, PE = nc.vector, nc.gpsimd, nc.scalar, nc.tensor
    ident, identf = make_ident(nc, S)
    mb = A("mb", [128, 256], F32)
    S.op("pool", lambda: G.memset(mb[:], 0.0), writes=["mb"])
    S.op("pool", lambda: G.affine_select(out=mb[:], in_=mb[:], pattern=[[1, 256]], compare_op=ALU.is_ge, fill=NEG, base=0,
                                         channel_multiplier=-1), reads=["mb"], writes=["mb"])
    S.op("pool", lambda: G.affine_select(out=mb[:], in_=mb[:], pattern=[[-1, 256]], compare_op=ALU.is_ge, fill=NEG, base=128,
                                         channel_multiplier=1), reads=["mb"], writes=["mb"])
    fl = A("fl", [128, NB], F32); fb = A("fb", [128, NB], F32)
    S.dma("sp", fl[:], fl_in[:, :], writes=["fl"], key="c")
    S.op("dve", lambda: V.tensor_scalar(out=fb[:], in0=fl[:], scalar1=-1.0, scalar2=-8.0 * NEG, op0=ALU.add, op1=ALU.mult), reads=["fl"], writes=["fb"])
    mb8 = A("mb8", [128, 256], F32)
    S.op("dve", lambda: V.tensor_scalar(out=mb8[:], in0=mb[:], scalar1=8.0, scalar2=None, op0=ALU.mult), reads=["mb"], writes=["mb8"])
    qb = [A("qb%d" % i, [128, 4, 128], BF16) for i in range(2)]; kb = [A("kb%d" % i, [128, 4, 256], BF16) for i in range(2)]
    vb = [A("vbb%d" % i, [128, 2, 512], BF16) for i in range(2)]; mbe = [A("mbe%d" % i, [128, 256], BF16) for i in range(2)]
    mx = [A("mx%d" % i, [128, 8], F32) for i in range(2)]; nmx = [A("nmx%d" % i, [128, 8], F32) for i in range(2)]
    ls = [A("ls%d" % i, [128, 8], F32) for i in range(2)]; ob = [A("ob%d" % i, [128, 512], F32) for i in range(2)]
    NR = 5
    prL = [A("pr%d" % i, [128, 256], BF16) for i in range(NR)]; mx8 = [A("mx8_%d" % i, [128, 8], F32) for i in range(2)]
    prTL = [A("prT%d" % i, [128, 2, 128], BF16) for i in range(NR)]
    NPS = 3
    ps = [nc.alloc_psum_tensor("ps%d" % i, [128, 256], F32) for i in range(NPS)]
    pt = [nc.alloc_psum_tensor("pt%d" % i, [128, 256], BF16) for i in range(2)]
    po = [nc.alloc_psum_tensor("po%d" % i, [128, 64], F32) for i in range(2)]

    def load_block(blk):
        b = blk % 2
        S.dma("sp", qb[b][:], qT_in[blk].rearrange("g p q -> p g q"), writes=["qb%d" % b], key="q%d" % b)
        S.dma("sp", kb[b][:], kT_in[blk].rearrange("g p k -> p g k"), writes=["kb%d" % b], key="k%d" % b)
        S.dma("sp", vb[b][:], v_in[blk].rearrange("c p f -> p c f"), writes=["vbb%d" % b], key="v%d" % b)
        S.op("pool", lambda: G.tensor_copy(out=mbe[b][:, 128:256], in_=mb8[:, 128:256]), reads=["mb8"], writes=["mbe_hi%d" % b])
        S.op("pool", lambda: G.tensor_scalar(out=mbe[b][:, 0:128], in0=mb8[:, 0:128], scalar1=fb[:, blk:blk + 1], scalar2=None, op0=ALU.add),
             reads=["mb8", "fb"], writes=["mbe_lo%d" % b])

    def names(i):
        blk, h = divmod(i, 8)
        b, ri, hb = blk % 2, i % NR, i % 2
        return blk, h, b, ri, hb

    def stage_a(i):
        blk, h, b, ri, hb = names(i)
        pi = i % NPS
        g2, o2 = h // 2, (h % 2) * 64
        S.op("pe", lambda: PE.matmul(ps[pi][:, :], lhsT=qb[b][o2:o2 + 64, g2, :], rhs=kb[b][o2:o2 + 64, g2, :], start=True, stop=False),
             reads=["qb%d" % b, "kb%d" % b], writes=["ps%d" % pi])
        S.op("pe", lambda: PE.matmul(ps[pi][:, :], lhsT=ident[:, :], rhs=mbe[b][:, :], start=False, stop=True),
             reads=["ident", "mbe_lo%d" % b, "mbe_hi%d" % b], writes=["ps%d" % pi])
        S.op("dve", lambda: V.tensor_reduce(out=mx8[b][:, h:h + 1], in_=ps[pi][:, :], axis=AX, op=ALU.max), reads=["ps%d" % pi], writes=["mx8_%d_%d" % (b, h)])
        S.op("dve", lambda: V.tensor_scalar(out=nmx[b][:, h:h + 1], in0=mx8[b][:, h:h + 1], scalar1=-0.125, scalar2=None, op0=ALU.mult),
             reads=["mx8_%d_%d" % (b, h)], writes=["nmx%d_%d" % (b, h)])

    def stage_b(i):
        blk, h, b, ri, hb = names(i)
        pi = i % NPS
        S.op("act", lambda: AC.activation(out=prL[ri][:], in_=ps[pi][:, :], func=AF.Exp, scale=0.125, bias=nmx[b][:, h:h + 1], accum_out=ls[b][:, h:h + 1]),
             reads=["ps%d" % pi, "nmx%d_%d" % (b, h)], writes=["pr%d" % ri, "ls%d_%d" % (b, h)])

    def stage_c(i):
        blk, h, b, ri, hb = names(i)
        for c2 in range(2):
            S.op("pe", lambda c2=c2: PE.transpose(out=pt[hb][:, c2 * 128:(c2 + 1) * 128], in_=prL[ri][:, c2 * 128:(c2 + 1) * 128], identity=ident[:]),
                 reads=["pr%d" % ri, "ident"], writes=["pt%d" % hb])
        S.op("act", lambda: AC.copy(out=prTL[ri][:], in_=pt[hb][:, :].rearrange("p (c q) -> p c q", c=2)), reads=["pt%d" % hb], writes=["prT%d" % ri])

    def stage_d(i):
        blk, h, b, ri, hb = names(i)
        for c2 in range(2):
            S.op("pe", lambda c2=c2: PE.matmul(po[hb][:, :], lhsT=prTL[ri][:, c2, :], rhs=vb[b][:, c2, h * 64:(h + 1) * 64], start=(c2 == 0), stop=(c2 == 1)),
                 reads=["prT%d" % ri, "vbb%d" % b], writes=["po%d" % hb])
        S.op("dve", lambda: V.tensor_copy(out=ob[b][:, h * 64:(h + 1) * 64], in_=po[hb][:, :]), reads=["po%d" % hb], writes=["ob%d_%d" % (b, h)])
        if h == 7:
            S.op("dve", lambda: V.tensor_scalar(out=mx[b][:], in0=mx8[b][:], scalar1=0.125, scalar2=None, op0=ALU.mult),
                 reads=["mx8_%d_%d" % (b, k) for k in range(8)], writes=["mx%d_%d" % (b, k) for k in range(8)])
            S.dma("pool", o_out[blk], ob[b][:], reads=["ob%d_%d" % (b, k) for k in range(8)], key="sto%d" % b)
            S.dma("pool", l_out[blk], ls[b][:], reads=["ls%d_%d" % (b, k) for k in range(8)], key="stl%d" % b)
            S.dma("pool", m_out[blk], mx[b][:], reads=["mx%d_%d" % (b, k) for k in range(8)], key="stm%d" % b)

    NI = NB * 8
    load_block(0)
    for s_ in range(NI + 3):
        if s_ % 8 == 3 and s_ // 8 + 1 < NB:
            load_block(s_ // 8 + 1)
        if s_ < NI:
            stage_a(s_)
        if 0 <= s_ - 1 < NI:
            stage_b(s_ - 1)
        if 0 <= s_ - 2 < NI:
            stage_c(s_ - 2)
        if 0 <= s_ - 3 < NI:
            stage_d(s_ - 3)
    S.wait_all("sp")
    return nc


PATTERN_DIL = (1, 4, 16)


def attn_layout(q_r, k_r, v):
    per_core = [dict() for _ in range(NCORE)]
    qTs, kTs, vs, fls, perm = [], [], [], [], []
    for d in PATTERN_DIL:
        L = S_TOT // d
        nb = L // 128
        tok = (np.arange(L)[None, :] * d + np.arange(d)[:, None])
        qs = q_r[tok].reshape(d, nb, 128, 8, 64)
        ks = k_r[tok].reshape(d, nb, 128, 8, 64)
        vv = v[tok].reshape(d, nb, 128, 512)
        kprev = np.concatenate([np.zeros_like(ks[:, :1]), ks[:, :-1]], axis=1)
        vprev = np.concatenate([np.zeros_like(vv[:, :1]), vv[:, :-1]], axis=1)
        k2 = np.concatenate([kprev, ks], axis=2).reshape(d * nb, 256, 8, 64)
        v2 = np.stack([vprev, vv], axis=2).reshape(d * nb, 2, 128, 512)
        flag = np.ones((d, nb), np.float32); flag[:, 0] = 0.0
        qT = qs.reshape(d * nb, 128, 4, 128).transpose(0, 2, 3, 1)
        kT = k2.reshape(d * nb, 256, 4, 128).transpose(0, 2, 3, 1)
        qTs.append(qT); kTs.append(kT); vs.append(v2); fls.append(flag.reshape(-1)); perm.append(tok.reshape(-1))
    maps = []
    for i in range(NCORE):
        sl = slice(i * NBLK, (i + 1) * NBLK)
        m = {
            "qT": np.ascontiguousarray(np.concatenate([a[sl] for a in qTs], axis=0)),
            "kT": np.ascontiguousarray(np.concatenate([a[sl] for a in kTs], axis=0)),
            "v": np.ascontiguousarray(np.concatenate([a[sl] for a in vs], axis=0)),
            "bflag": np.ascontiguousarray(np.broadcast_to(np.concatenate([f[sl] for f in fls])[None, :], (128, 3 * NBLK))),
        }
        maps.append(m)
    return maps, perm


def run_attn(q_r, k_r, v):
    maps, perm = attn_layout(q_r, k_r, v)
    res = run_bass_kernel_spmd(build_attn(), maps, core_ids=list(range(NCORE)))
    outs = []
    for p in range(3):
        o = np.concatenate([r["o_un"][p * NBLK:(p + 1) * NBLK] for r in res.results], axis=0).reshape(S_TOT, 512)
        l = np.concatenate([r["l_un"][p * NBLK:(p + 1) * NBLK] for r in res.results], axis=0).reshape(S_TOT, 8)
        m = np.concatenate([r["m_un"][p * NBLK:(p + 1) * NBLK] for r in res.results], axis=0).reshape(S_TOT, 8)
        inv = np.empty(S_TOT, np.int64); inv[perm[p]] = np.arange(S_TOT)
        outs.append((np.ascontiguousarray(o[inv]), np.ascontiguousarray(l[inv]), np.ascontiguousarray(m[inv])))
    return outs


def build_final():
    nc = bass.Bass("TRN2", target_bir_lowering=False)
    x_in = nc.dram_tensor("x", [TOK, D], F32, kind="ExternalInput").ap()
    odn_in = nc.dram_tensor("o_dn", [TOK, 512], F32, kind="ExternalInput").ap()
    az_in = nc.dram_tensor("at_z", [TOK, 512], F32, kind="ExternalInput").ap()
    o_in = [nc.dram_tensor("ao%d" % p, [TOK, 512], F32, kind="ExternalInput").ap() for p in range(3)]
    l_in = [nc.dram_tensor("al%d" % p, [TOK, 8], F32, kind="ExternalInput").ap() for p in range(3)]
    m_in = [nc.dram_tensor("am%d" % p, [TOK, 8], F32, kind="ExternalInput").ap() for p in range(3)]
    wout_in = nc.dram_tensor("w_out", [D, D], F32, kind="ExternalInput").ap()
    nwc_in = nc.dram_tensor("mixnw", [128, 8], F32, kind="ExternalInput").ap()
    gate_in = nc.dram_tensor("gate_b", [128, D], F32, kind="ExternalInput").ap()
    fnw_in = nc.dram_tensor("fnw_b", [128, D], F32, kind="ExternalInput").ap()
    y_out = nc.dram_tensor("y", [TOK, D], F32, kind="ExternalOutput").ap()
    S = Sched(nc)
    A = nc.alloc_sbuf_tensor
    V, G, AC, PE = nc.vector, nc.gpsimd, nc.scalar, nc.tensor
    ident, identf = make_ident(nc, S)
    nwc = A("nwc", [128, 8], F32); gate = A("gate", [128, D], F32); fnw = A("fnw", [128, D], F32)
    S.dma("sp", nwc[:], nwc_in[:, :], writes=["nwc"], key="c")
    S.dma("sp", gate[:], gate_in[:, :], writes=["gate"], key="c")
    S.dma("sp", fnw[:], fnw_in[:, :], writes=["fnw"], key="c")
    wst = A("wst", [128, 8, D], F32); wo = A("wo", [128, 8, D], BF16)
    S.dma("act", wst[:], wout_in.rearrange("(kt p) m -> p kt m", p=128), writes=["wst"], key="w")
    for kt in range(8):
        if kt % 2 == 0:
            S.op("dve", lambda kt=kt: V.tensor_scalar(out=wo[:, kt, :], in0=wst[:, kt, :], scalar1=nwc[:, kt:kt + 1], scalar2=None, op0=ALU.mult),
                 reads=["wst", "nwc"], writes=["wo%d" % kt])
        else:
            S.op("act", lambda kt=kt: AC.activation(out=wo[:, kt, :], in_=wst[:, kt, :], func=AF.Copy, scale=nwc[:, kt:kt + 1]),
                 reads=["wst", "nwc"], writes=["wo%d" % kt])
    H8 = [128, 8, 64]
    xtL = [A("xt%d" % i, [128, D], F32) for i in range(2)]; odL = [A("od%d" % i, [128, 512], F32) for i in range(2)]
    azL = [A("az%d" % i, [128, 512], F32) for i in range(2)]
    aoL = [[A("aot%d_%d" % (p, i), H8, F32) for p in range(3)] for i in range(2)]
    alL = [[A("alt%d_%d" % (p, i), [128, 8], F32) for p in range(3)] for i in range(2)]
    amL = [[A("amt%d_%d" % (p, i), [128, 8], F32) for p in range(3)] for i in range(2)]

    def load_tile(t):
        i = t % 2
        ts = slice(t * 128, (t + 1) * 128)
        S.dma("sp", xtL[i][:], x_in[ts, :], writes=["xt%d" % i], key="x%d" % i)
        S.dma("act", odL[i][:], odn_in[ts, :], writes=["od%d" % i], key="od%d" % i)
        S.dma("act", azL[i][:], az_in[ts, :], writes=["az%d" % i], key="az%d" % i)
        for p in range(3):
            S.dma("sp", aoL[i][p][:], o_in[p][ts, :].rearrange("p (h e) -> p h e", h=8), writes=["ao%d_%d" % (p, i)], key="ao%d_%d" % (p, i))
            S.dma("pool", alL[i][p][:], l_in[p][ts, :], writes=["al%d_%d" % (p, i)], key="al%d_%d" % (p, i))
            S.dma("pool", amL[i][p][:], m_in[p][ts, :], writes=["am%d_%d" % (p, i)], key="am%d_%d" % (p, i))
    mm_ = A("mm_", [128, 8], F32); wp = [A("wp%d" % p, [128, 8], F32) for p in range(3)]; lt = A("lt", [128, 8], F32)
    oa = A("oa", H8, F32); tmp8 = A("tmp8", H8, F32); sqa = A("sqa", H8, F32); ssa = A("ssa", [128, 8], F32); ra = A("ra", [128, 8], F32)
    mixbL = [A("mixb%d" % i, [128, D], BF16) for i in range(2)]; mixT = A("mixT", [128, 8, 128], BF16); yo = A("yo", [128, D], F32); junk = A("junkf", [128, D], F32)
    ssf = A("ssf", [128, 1], F32)
    pT = nc.alloc_psum_tensor("pT", [128, 8, 128], BF16)
    pm = [nc.alloc_psum_tensor("pm%d" % i, [128, 512], F32) for i in range(2)]

    def bc8(a):
        return a.unsqueeze(2).to_broadcast(H8)
    def stage_a(t):
        i_ = t % 2
        od, az, ao, al, am, mixb = odL[i_], azL[i_], aoL[i_], alL[i_], amL[i_], mixbL[i_]
        kod, kaz = "od%d" % i_, "az%d" % i_
        kao = ["ao%d_%d" % (p, i_) for p in range(3)]; kal = ["al%d_%d" % (p, i_) for p in range(3)]; kam = ["am%d_%d" % (p, i_) for p in range(3)]
        kmlo, kmhi = "mixb_lo%d" % i_, "mixb_hi%d" % i_
        S.op("dve", lambda: V.tensor_tensor(out=mm_[:], in0=am[0][:], in1=am[1][:], op=ALU.max), reads=[kam[0], kam[1]], writes=["mm_"])
        S.op("dve", lambda: V.tensor_tensor(out=mm_[:], in0=mm_[:], in1=am[2][:], op=ALU.max), reads=["mm_", kam[2]], writes=["mm_"])
        for p in range(3):
            S.op("dve", lambda p=p: V.tensor_tensor(out=wp[p][:], in0=am[p][:], in1=mm_[:], op=ALU.subtract), reads=[kam[p], "mm_"], writes=["wp%d" % p])
            S.op("act", lambda p=p: AC.activation(out=wp[p][:], in_=wp[p][:], func=AF.Exp), reads=["wp%d" % p], writes=["wp%d" % p])
        S.op("dve", lambda: V.tensor_tensor(out=lt[:], in0=al[0][:], in1=wp[0][:], op=ALU.mult), reads=[kal[0], "wp0"], writes=["lt"])
        S.op("dve", lambda: V.tensor_tensor(out=oa[:], in0=ao[0][:], in1=bc8(wp[0][:, :]), op=ALU.mult), reads=[kao[0], "wp0"], writes=["oa"])
        for p in (1, 2):
            S.op("dve", lambda p=p: V.tensor_tensor(out=ra[:], in0=al[p][:], in1=wp[p][:], op=ALU.mult), reads=[kal[p], "wp%d" % p], writes=["ra"])
            S.op("dve", lambda: V.tensor_tensor(out=lt[:], in0=lt[:], in1=ra[:], op=ALU.add), reads=["lt", "ra"], writes=["lt"])
            S.op("pool", lambda p=p: G.tensor_tensor(out=tmp8[:], in0=ao[p][:], in1=bc8(wp[p][:, :]), op=ALU.mult), reads=[kao[p], "wp%d" % p], writes=["tmp8"])
            S.op("dve", lambda: V.tensor_tensor(out=oa[:], in0=oa[:], in1=tmp8[:], op=ALU.add), reads=["oa", "tmp8"], writes=["oa"])
        S.op("dve", lambda: V.reciprocal(out=lt[:], in_=lt[:]), reads=["lt"], writes=["lt"])
        S.op("dve", lambda: V.tensor_tensor(out=oa[:], in0=oa[:], in1=bc8(lt[:, :]), op=ALU.mult), reads=["oa", "lt"], writes=["oa"])
        S.op("act", lambda: AC.activation(out=sqa[:], in_=oa[:], func=AF.Square), reads=["oa"], writes=["sqa"])
        S.op("dve", lambda: V.tensor_reduce(out=ssa[:], in_=sqa[:], axis=AX, op=ALU.add), reads=["sqa"], writes=["ssa"])
        S.op("dve", lambda: V.tensor_scalar(out=ssa[:], in0=ssa[:], scalar1=1.0 / 64, scalar2=EPS, op0=ALU.mult, op1=ALU.add), reads=["ssa"], writes=["ssa"])
        S.op("act", lambda: AC.activation(out=ssa[:], in_=ssa[:], func=AF.Sqrt), reads=["ssa"], writes=["ssa"])
        S.op("dve", lambda: V.reciprocal(out=ra[:], in_=ssa[:]), reads=["ssa"], writes=["ra"])
        S.op("dve", lambda: V.tensor_tensor(out=oa[:], in0=oa[:], in1=bc8(ra[:, :]), op=ALU.mult), reads=["oa", "ra"], writes=["oa"])
        S.op("act", lambda: AC.activation(out=az[:], in_=az[:], func=AF.Silu), reads=[kaz], writes=[kaz])
        S.op("dve", lambda: V.tensor_tensor(out=mixb[:, 512:1024], in0=oa[:].rearrange("p h e -> p (h e)"), in1=az[:], op=ALU.mult),
             reads=["oa", kaz], writes=[kmhi])
        S.op("act", lambda: AC.copy(out=mixb[:, 0:512], in_=od[:]), reads=[kod], writes=[kmlo])

    def stage_b(t):
        i_ = t % 2
        ts = slice(t * 128, (t + 1) * 128)
        xt, mixb, kx = xtL[i_], mixbL[i_], "xt%d" % i_
        kmlo, kmhi = "mixb_lo%d" % i_, "mixb_hi%d" % i_
        for kt in range(8):
            S.op("pe", lambda kt=kt: PE.transpose(out=pT[:, kt, :], in_=mixb[:, kt * 128:(kt + 1) * 128], identity=ident[:]),
                 reads=[kmlo, kmhi, "ident"], writes=["pT"])
        S.op("act", lambda: AC.copy(out=mixT[:], in_=pT[:]), reads=["pT"], writes=["mixT"])
        for cg in range(2):
            for kt in range(8):
                S.op("pe", lambda cg=cg, kt=kt: PE.matmul(pm[cg][:, :], lhsT=mixT[:, kt, :], rhs=wo[:, kt, cg * 512:(cg + 1) * 512], start=(kt == 0), stop=(kt == 7)),
                     reads=["mixT"] + ["wo%d" % k for k in range(8)], writes=["pm%d" % cg])
            cs_ = slice(cg * 512, (cg + 1) * 512)
            S.op("dve", lambda cg=cg, cs_=cs_: V.tensor_tensor(out=yo[:, cs_], in0=pm[cg][:, :], in1=gate[:, cs_], op=ALU.mult), reads=["pm%d" % cg, "gate"], writes=["yo%d" % cg])
            S.op("dve", lambda cs_=cs_: V.tensor_tensor(out=yo[:, cs_], in0=yo[:, cs_], in1=xt[:, cs_], op=ALU.add), reads=["yo%d" % cg, kx], writes=["yo%d" % cg])
        S.op("act", lambda: AC.activation(out=junk[:], in_=yo[:], func=AF.Square, accum_out=ssf[:]), reads=["yo0", "yo1"], writes=["junkf", "ssf"])
        S.op("dve", lambda: V.tensor_scalar(out=ssf[:], in0=ssf[:], scalar1=1.0 / D, scalar2=EPS, op0=ALU.mult, op1=ALU.add), reads=["ssf"], writes=["ssf"])
        S.op("act", lambda: AC.activation(out=ssf[:], in_=ssf[:], func=AF.Sqrt), reads=["ssf"], writes=["ssf"])
        S.op("dve", lambda: V.reciprocal(out=ssf[:], in_=ssf[:]), reads=["ssf"], writes=["ssf"])
        S.op("dve", lambda: V.scalar_tensor_tensor(out=yo[:], in0=yo[:], scalar=ssf[:, 0:1], in1=fnw[:], op0=ALU.mult, op1=ALU.mult),
             reads=["yo0", "yo1", "ssf", "fnw"], writes=["yo0", "yo1"])
        S.dma("sp", y_out[ts, :], yo[:], reads=["yo0", "yo1"], key="st")

    load_tile(0)
    if NT > 1:
        load_tile(1)
    stage_a(0)
    for t in range(NT):
        if t + 1 < NT:
            stage_a(t + 1)
        stage_b(t)
        if t + 2 < NT:
            load_tile(t + 2)
    S.wait_all("sp")
    return nc


def run_all(x, c, positions, w_mod, b_mod, norm_w, w_in, conv_w, a_log, dt_bias, dn_norm_w, at_norm_w, w_out, final_norm_w):
    cores = list(range(NCORE))
    proj, modT, (q_r, k_r, v_b) = run_l1(x, c, w_mod, b_mod, norm_w, w_in, positions)
    o_dn, _ = run_dn(proj, conv_w, a_log, dt_bias)
    pats = run_attn(q_r, k_r, v_b)
    gate_row = modT.T.reshape(-1)[2048:3072]
    mixnw = np.concatenate([np.tile(dn_norm_w[0], 4), np.tile(at_norm_w[0], 8)]).astype(np.float32)
    maps = []
    for i in cores:
        sl = slice(i * TOK, (i + 1) * TOK)
        m = {"x": np.ascontiguousarray(x[0, sl]), "o_dn": np.ascontiguousarray(o_dn[sl]), "at_z": np.ascontiguousarray(proj[sl, 3592:4104]),
             "w_out": np.ascontiguousarray(w_out[0]), "mixnw": np.ascontiguousarray(mixnw.reshape(8, 128).T),
             "gate_b": np.ascontiguousarray(np.broadcast_to(gate_row[None, :], (128, D))),
             "fnw_b": np.ascontiguousarray(np.broadcast_to(final_norm_w[None, :], (128, D)))}
        for p in range(3):
            m["ao%d" % p], m["al%d" % p], m["am%d" % p] = (np.ascontiguousarray(a[sl]) for a in pats[p])
        maps.append(m)
    rf = run_bass_kernel_spmd(build_final(), maps, core_ids=cores)
    y = np.concatenate([r["y"] for r in rf.results], axis=0)
    return y.reshape(1, S_TOT, D).astype(np.float32)


def kernel(**inputs):
    return run_all(**{k: np.asarray(v) for k, v in inputs.items()})
```
